# Optimizing a Trainium2 kernel written in Bass

```python
import math
import jax, jax.numpy as jnp
from jax import lax
import numpy as np

D_MODEL = 1024
BATCH = 16
SEQ = 4096
DEPTH = 1

D_MIX = 2 * D_MODEL
SSD_HEADDIM = 64
D_SSD = 3 * D_MIX // 4
SSD_HEADS = D_SSD // SSD_HEADDIM
SSD_GROUPS = 4
SSD_HPG = SSD_HEADS // SSD_GROUPS
SSD_STATE = 128
CONV_WIDTH = 4
CHUNK = 128
D_XBC = D_SSD + 2 * SSD_GROUPS * SSD_STATE
D_S5 = D_MIX - D_SSD
S5_CH = 16
S5_GROUPS = D_S5 // S5_CH
S5_STATE = 64
D_IN_PROJ = D_SSD + D_XBC + SSD_HEADS + D_S5
D_FF = 4 * D_MODEL
ALPHA = (2.0 * DEPTH) ** 0.25
BETA = (8.0 * DEPTH) ** -0.25
EPS = 1e-5
N_MOD = 6

kernel_name = 'hymba_ssd_s5_deepnorm_adaln_layer'


def layer_norm(x, g, b):
    xf = x.astype(jnp.float32)
    mu = jnp.mean(xf, axis=-1, keepdims=True)
    xc = xf - mu
    var = jnp.mean(xc * xc, axis=-1, keepdims=True)
    y = xc * lax.rsqrt(var + EPS) * g.astype(jnp.float32) + b.astype(jnp.float32)
    return y.astype(x.dtype)


def causal_dwconv(x, w, b):
    ch = x.shape[-1]
    y = lax.conv_general_dilated(x, w[:, None, :].astype(x.dtype), window_strides=(1,),
                                 padding=[(CONV_WIDTH - 1, 0)],
                                 dimension_numbers=('NWC', 'WIO', 'NWC'),
                                 feature_group_count=ch)
    return y + b.astype(x.dtype)


def ssd_chunked(x, dt, a, bmat, cmat):
    bsz, seqlen = x.shape[0], x.shape[1]
    nc = seqlen // CHUNK
    xdt = (x * dt[..., None]).reshape(bsz, nc, CHUNK, SSD_GROUPS, SSD_HPG, SSD_HEADDIM)
    adt = (dt * a).reshape(bsz, nc, CHUNK, SSD_GROUPS, SSD_HPG)
    adt = jnp.transpose(adt, (0, 3, 4, 1, 2))
    bmat = bmat.reshape(bsz, nc, CHUNK, SSD_GROUPS, SSD_STATE)
    cmat = cmat.reshape(bsz, nc, CHUNK, SSD_GROUPS, SSD_STATE)
    a_cs = jnp.cumsum(adt, axis=-1)
    causal = jnp.tril(jnp.ones((CHUNK, CHUNK), dtype=bool))
    seg = a_cs[..., :, None] - a_cs[..., None, :]
    decay = jnp.exp(jnp.where(causal, seg, -jnp.inf))
    scores = jnp.einsum('bclgn,bcsgn->bgcls', cmat, bmat)
    y_diag = jnp.einsum('bgcls,bgjcls,bcsgjp->bclgjp', scores, decay, xdt)
    decay_to_end = jnp.exp(a_cs[..., -1:] - a_cs)
    states = jnp.einsum('bclgn,bgjcl,bclgjp->bcgjpn', bmat, decay_to_end, xdt)
    chunk_decay = jnp.exp(a_cs[..., -1])

    def step(h, inp):
        s_c, d_c = inp
        return h * d_c[..., None, None] + s_c, h

    h0 = jnp.zeros((bsz, SSD_GROUPS, SSD_HPG, SSD_HEADDIM, SSD_STATE), jnp.float32)
    _, prev = lax.scan(step, h0, (jnp.moveaxis(states, 1, 0), jnp.moveaxis(chunk_decay, -1, 0)))
    y_off = jnp.einsum('bclgn,cbgjpn,bgjcl->bclgjp', cmat, prev, jnp.exp(a_cs))
    return (y_diag + y_off).reshape(bsz, seqlen, SSD_HEADS, SSD_HEADDIM)


def ssd_mixer(zxbcdt, conv_w, conv_b, dt_bias, a_log, d_skip, norm_w):
    bsz, seqlen = zxbcdt.shape[0], zxbcdt.shape[1]
    z = zxbcdt[..., :D_SSD]
    xbc = zxbcdt[..., D_SSD:D_SSD + D_XBC]
    dt_raw = zxbcdt[..., D_SSD + D_XBC:]
    xbc = jax.nn.silu(causal_dwconv(xbc, conv_w, conv_b)).astype(jnp.float32)
    xs = xbc[..., :D_SSD].reshape(bsz, seqlen, SSD_HEADS, SSD_HEADDIM)
    bm = xbc[..., D_SSD:D_SSD + SSD_GROUPS * SSD_STATE].reshape(bsz, seqlen, SSD_GROUPS, SSD_STATE)
    cm = xbc[..., D_SSD + SSD_GROUPS * SSD_STATE:].reshape(bsz, seqlen, SSD_GROUPS, SSD_STATE)
    dt = jax.nn.softplus(dt_raw.astype(jnp.float32) + dt_bias.astype(jnp.float32))
    a = -jnp.exp(a_log.astype(jnp.float32))
    y = ssd_chunked(xs, dt, a, bm, cm) + xs * d_skip.astype(jnp.float32)[:, None]
    y = y.reshape(bsz, seqlen, D_SSD) * jax.nn.silu(z.astype(jnp.float32))
    yg = y.reshape(bsz, seqlen, SSD_GROUPS, D_SSD // SSD_GROUPS)
    yg = yg * lax.rsqrt(jnp.mean(yg * yg, axis=-1, keepdims=True) + EPS)
    return (yg.reshape(bsz, seqlen, D_SSD) * norm_w.astype(jnp.float32)).astype(zxbcdt.dtype)


def complex_affine_combine(e1, e2):
    a1r, a1i, b1r, b1i = e1
    a2r, a2i, b2r, b2i = e2
    return (a2r * a1r - a2i * a1i,
            a2r * a1i + a2i * a1r,
            a2r * b1r - a2i * b1i + b2r,
            a2r * b1i + a2i * b1r + b2i)


def s5_mixer(u, a_re, a_im, log_dt, b_re, b_im, c_re, c_im, d_skip, w_glu, b_glu):
    f32 = jnp.float32
    bsz, seqlen = u.shape[0], u.shape[1]
    uf = u.astype(f32).reshape(bsz, seqlen, S5_GROUPS, S5_CH)
    ar = a_re.astype(f32)
    ai = a_im.astype(f32)
    dt = jnp.exp(log_dt.astype(f32))[:, None]
    mag = jnp.exp(ar * dt)
    ang = ai * dt
    ab_re = mag * jnp.cos(ang)
    ab_im = mag * jnp.sin(ang)
    den = ar * ar + ai * ai
    n_re = ab_re - 1.0
    coef_re = (n_re * ar + ab_im * ai) / den
    coef_im = (ab_im * ar - n_re * ai) / den
    br = b_re.astype(f32)
    bi = b_im.astype(f32)
    bb_re = coef_re[..., None] * br - coef_im[..., None] * bi
    bb_im = coef_re[..., None] * bi + coef_im[..., None] * br
    bu_re = jnp.einsum('bsgh,gph->bsgp', uf, bb_re)
    bu_im = jnp.einsum('bsgh,gph->bsgp', uf, bb_im)
    shape = (1, seqlen, S5_GROUPS, S5_STATE)
    elems = (jnp.broadcast_to(ab_re, shape), jnp.broadcast_to(ab_im, shape), bu_re, bu_im)
    _, _, s_re, s_im = lax.associative_scan(complex_affine_combine, elems, axis=1)
    y = (jnp.einsum('bsgp,ghp->bsgh', s_re, c_re.astype(f32))
         - jnp.einsum('bsgp,ghp->bsgh', s_im, c_im.astype(f32))
         + uf * d_skip.astype(f32))
    y = jax.nn.gelu(y.reshape(bsz, seqlen, D_S5))
    y = y * jax.nn.sigmoid(y @ w_glu.astype(f32) + b_glu.astype(f32))
    return y.astype(u.dtype)


def setup_inputs(seed: int = 0) -> dict:
    key = jax.random.key(seed)
    ks = jax.random.split(key, 32)
    nrm = lambda k, shp: jax.random.normal(k, shp, jnp.float32)
    L = DEPTH
    dt0 = jnp.exp(jax.random.uniform(ks[6], (L, SSD_HEADS), jnp.float32, math.log(1e-3), math.log(1e-1)))
    inputs = {
        'x': nrm(ks[0], (BATCH, SEQ, D_MODEL)),
        'c': nrm(ks[1], (BATCH, D_MODEL)),
        'w_ada': nrm(ks[2], (L, D_MODEL, N_MOD * D_MODEL)) * (0.5 * D_MODEL ** -0.5),
        'b_ada': 0.01 * nrm(ks[3], (L, N_MOD * D_MODEL)),
        'w_in': nrm(ks[4], (L, D_MODEL, D_IN_PROJ)) * D_MODEL ** -0.5,
        'conv_w': nrm(ks[5], (L, CONV_WIDTH, D_XBC)) * CONV_WIDTH ** -0.5,
        'conv_b': 0.01 * nrm(ks[7], (L, D_XBC)),
        'dt_bias': dt0 + jnp.log(-jnp.expm1(-dt0)),
        'a_log': jnp.log(jax.random.uniform(ks[8], (L, SSD_HEADS), jnp.float32, 1.0, 16.0)),
        'd_ssd': 1.0 + 0.01 * nrm(ks[9], (L, SSD_HEADS)),
        'norm_w': 1.0 + 0.01 * nrm(ks[10], (L, D_SSD)),
        's5_a_re': -0.5 + 0.01 * nrm(ks[11], (L, S5_GROUPS, S5_STATE)),
        's5_a_im': jnp.pi * jnp.arange(S5_STATE, dtype=jnp.float32) + 0.01 * nrm(ks[12], (L, S5_GROUPS, S5_STATE)),
        's5_log_dt': jax.random.uniform(ks[13], (L, S5_GROUPS), jnp.float32, math.log(1e-3), math.log(1e-1)),
        's5_b_re': nrm(ks[14], (L, S5_GROUPS, S5_STATE, S5_CH)) * (2 * S5_CH) ** -0.5,
        's5_b_im': nrm(ks[15], (L, S5_GROUPS, S5_STATE, S5_CH)) * (2 * S5_CH) ** -0.5,
        's5_c_re': nrm(ks[16], (L, S5_GROUPS, S5_CH, S5_STATE)) * S5_STATE ** -0.5,
        's5_c_im': nrm(ks[17], (L, S5_GROUPS, S5_CH, S5_STATE)) * S5_STATE ** -0.5,
        's5_d': nrm(ks[18], (L, S5_GROUPS, S5_CH)),
        'w_glu': nrm(ks[19], (L, D_S5, D_S5)) * D_S5 ** -0.5,
        'b_glu': 0.01 * nrm(ks[20], (L, D_S5)),
        'w_out': nrm(ks[21], (L, D_MIX, D_MODEL)) * (D_MIX ** -0.5 * BETA),
        'ln1_g': 1.0 + 0.01 * nrm(ks[22], (L, D_MODEL)),
        'ln1_b': 0.01 * nrm(ks[23], (L, D_MODEL)),
        'w1': nrm(ks[24], (L, D_MODEL, D_FF)) * D_MODEL ** -0.5,
        'b1': 0.01 * nrm(ks[25], (L, D_FF)),
        'w2': nrm(ks[26], (L, D_FF, D_MODEL)) * (D_FF ** -0.5 * BETA),
        'b2': 0.01 * nrm(ks[27], (L, D_MODEL)),
        'ln2_g': 1.0 + 0.01 * nrm(ks[28], (L, D_MODEL)),
        'ln2_b': 0.01 * nrm(ks[29], (L, D_MODEL)),
    }
    return inputs


def reference(x, c, w_ada, b_ada, w_in, conv_w, conv_b, dt_bias, a_log, d_ssd, norm_w,
              s5_a_re, s5_a_im, s5_log_dt, s5_b_re, s5_b_im, s5_c_re, s5_c_im, s5_d,
              w_glu, b_glu, w_out, ln1_g, ln1_b, w1, b1, w2, b2, ln2_g, ln2_b):
    cond = jax.nn.silu(c)
    for l in range(DEPTH):
        mod = (cond @ w_ada[l] + b_ada[l])[:, None, :]
        sh1, sc1, g1, sh2, sc2, g2 = jnp.split(mod, N_MOD, axis=-1)
        u = x * (1.0 + sc1) + sh1
        proj = u @ w_in[l]
        y_ssd = ssd_mixer(proj[..., :D_IN_PROJ - D_S5], conv_w[l], conv_b[l],
                          dt_bias[l], a_log[l], d_ssd[l], norm_w[l])
        y_s5 = s5_mixer(proj[..., D_IN_PROJ - D_S5:], s5_a_re[l], s5_a_im[l], s5_log_dt[l],
                        s5_b_re[l], s5_b_im[l], s5_c_re[l], s5_c_im[l], s5_d[l],
                        w_glu[l], b_glu[l])
        mix = jnp.concatenate([y_ssd, y_s5], axis=-1) @ w_out[l]
        x = layer_norm(ALPHA * x + (1.0 + g1) * mix, ln1_g[l], ln1_b[l])
        u = x * (1.0 + sc2) + sh2
        h = jnp.square(jax.nn.relu(u @ w1[l] + b1[l]))
        x = layer_norm(ALPHA * x + (1.0 + g2) * (h @ w2[l] + b2[l]), ln2_g[l], ln2_b[l])
    return x
```

```python
import contextlib
import math
import numpy as np
import concourse.bass as bass
import concourse.mybir as mybir
from concourse.bass_utils import run_bass_kernel_spmd

F32 = mybir.dt.float32
BF16 = mybir.dt.bfloat16
I32 = mybir.dt.int32
AF = mybir.ActivationFunctionType
ALU = mybir.AluOpType

D = 1024
DSSD = 1536
NH = 24
DIN = 4632
DFF = 4096
ALPHA = 2.0 ** 0.25
EPS = 1e-5
TWO_PI = 2.0 * math.pi
SAME_ENG_SYNC = True


class _Op:
    __slots__ = ("eng", "fn", "reads", "writes", "dma", "chan", "waits", "signal", "val", "idx")

    def __init__(self, eng, fn, reads, writes, dma, chan):
        self.eng = eng
        self.fn = fn
        self.reads = reads
        self.writes = writes
        self.dma = dma
        self.chan = chan
        self.waits = []
        self.signal = False
        self.val = None


class Sched:
    ENGS = ("pe", "act", "dve", "pool", "sp")

    def __init__(self, nc):
        self.nc = nc
        self.ops = []

    def add(self, eng, fn, reads=(), writes=(), dma=False, chan=None):
        op = _Op(eng, fn, tuple(reads), tuple(writes), dma, chan)
        op.idx = len(self.ops)
        self.ops.append(op)
        return op

    def dma(self, eng, fn, reads=(), writes=(), chan=None):
        if chan is None:
            chan = ("auto", tuple(writes) if writes else tuple(reads))
        return self.add(eng, fn, reads, writes, dma=True, chan=chan)

    def barrier(self, mk, extra_reads=()):
        n = getattr(self, "_bar_n", 0)
        self._bar_n = n + 1
        names = []
        for E in ("pe", "act", "dve", "pool"):
            nm = "bar%d_%s" % (n, E)
            names.append(nm)
            self.add(E, mk(E, 0), (["ones_f"] if E == "act" else ["ones_bf"] if E == "pe" else []), [nm] + (["ps7"] if E == "pe" else []))
        for E in ("pe", "act", "dve", "pool"):
            self.add(E, mk(E, 1), names + list(extra_reads) + (["ones_f"] if E == "act" else ["ones_bf"] if E == "pe" else []),
                     ["bar%d_b_%s" % (n, E)] + (["ps7"] if E == "pe" else []))
        self.dma("sp", mk("sp", 1), names + list(extra_reads), ["bar_d"], chan=("bar", n))

    def finalize(self):
        last_w = {}
        readers = {}
        deps_all = []
        for op in self.ops:
            deps = set()
            for r in op.reads:
                w = last_w.get(r)
                if w is not None:
                    deps.add(w)
            for r in op.writes:
                w = last_w.get(r)
                if w is not None:
                    deps.add(w)
                for rd in readers.get(r, ()):
                    deps.add(rd)
            deps.discard(op.idx)
            for r in op.reads:
                readers.setdefault(r, []).append(op.idx)
            for r in op.writes:
                last_w[r] = op.idx
                readers[r] = []
            need = []
            for d in sorted(deps):
                Dp = self.ops[d]
                if (not Dp.dma) and Dp.eng == op.eng:
                    if op.eng == "pe" or not SAME_ENG_SYNC:
                        continue
                need.append(d)
            deps_all.append(need)
            for d in need:
                self.ops[d].signal = True
        cnt = {e: 0 for e in self.ENGS}
        chan_cnt = {}
        for op in self.ops:
            if op.dma:
                c = chan_cnt.get(op.chan, 0) + 1
                chan_cnt[op.chan] = c
                op.val = 16 * c
            elif op.signal:
                cnt[op.eng] += 1
                op.val = cnt[op.eng]
        self.chans = list(chan_cnt.keys())
        waited = {e: {} for e in self.ENGS}
        for op, need in zip(self.ops, deps_all):
            for d in need:
                Dp = self.ops[d]
                key = ("c", Dp.chan) if Dp.dma else ("e", Dp.eng)
                if waited[op.eng].get(key, 0) >= Dp.val:
                    continue
                waited[op.eng][key] = Dp.val
                op.waits.append((key, Dp.val))
        return cnt, chan_cnt

    def emit(self, final_wait_chans=()):
        nc = self.nc
        with contextlib.ExitStack() as st:
            sems = {}
            for e in self.ENGS:
                sems[("e", e)] = st.enter_context(nc.semaphore("s_" + e))
            for i, c in enumerate(self.chans):
                sems[("c", c)] = st.enter_context(nc.semaphore("d%d" % i))
            block = st.enter_context(nc.Block())
            by_eng = {e: [op for op in self.ops if op.eng == e] for e in self.ENGS}
            chan_final = {}
            for op in self.ops:
                if op.dma:
                    chan_final[op.chan] = op.val

            def run(engobj, ename):
                for op in by_eng[ename]:
                    for key, val in op.waits:
                        engobj.wait_ge(sems[key], val)
                    ins = op.fn(engobj)
                    if op.dma:
                        ins.then_inc(sems[("c", op.chan)], 16)
                    elif op.signal:
                        ins.then_inc(sems[("e", ename)], 1)
                if ename == "sp":
                    for c in final_wait_chans:
                        engobj.wait_ge(sems[("c", c)], chan_final[c])

            @block.tensor
            def _(e):
                run(e, "pe")

            @block.scalar
            def _(e):
                run(e, "act")

            @block.vector
            def _(e):
                run(e, "dve")

            @block.gpsimd
            def _(e):
                run(e, "pool")

            @block.sync
            def _(e):
                run(e, "sp")


def build_program(NSB=8, dbg=(), phases=("A1", "A2", "B")):
    SEQ = NSB * 512
    nc = bass.Bass("TRN2", target_bir_lowering=False)
    S = Sched(nc)
    _uid = iter(range(100000))

    def din(name, shape, dt=F32):
        return nc.dram_tensor(name, list(shape), dt, kind="ExternalInput").ap()

    def dscr(name, shape, dt=F32):
        return nc.dram_tensor(name, list(shape), dt, kind="Internal").ap()

    xT_d = din("xT", [2, D, SEQ])
    x_d = din("x", [2, SEQ, D])
    cT_d = din("cT", [128, 8, 2])
    w_ada_d = din("w_ada", [D, 6 * D])
    b_ada_d = din("b_ada", [1, 6 * D])
    b_adaT_d = din("b_adaT", [128, 16])
    w_in_d = din("w_in", [D, DIN])
    convwT_d = din("convwT", [128, 20, 4])
    convb_d = din("conv_b", [1, 2560])
    convbT_d = din("convbT", [128, 20])
    dtb_d = din("dt_bias", [1, NH])
    alog_d = din("a_log", [1, NH])
    dfull_d = din("dfull", [1, DSSD])
    dssd_d = din("d_ssd", [1, NH])
    normwT_d = din("normwT", [128, 12])
    s5ar_d = din("s5ar", [128, 16])
    s5ai_d = din("s5ai", [128, 16])
    s5ldt_d = din("s5ldt", [128, 16])
    s5br_d = din("s5br", [128, 16, 16])
    s5bi_d = din("s5bi", [128, 16, 16])
    s5cr_d = din("s5cr", [128, 16, 16])
    s5ci_d = din("s5ci", [128, 16, 16])
    s5dv_d = din("s5dv", [128, 32])
    w_glu_d = din("w_glu", [512, 512])
    b_gluT_d = din("b_gluT", [128, 4])
    w_out_d = din("w_out", [2048, D])
    ln1g_d = din("ln1_g", [1, D])
    ln1b_d = din("ln1_b", [1, D])
    w1_d = din("w1", [D, DFF])
    b1T_d = din("b1T", [128, 32])
    w2_d = din("w2", [DFF, D])
    b2_d = din("b2", [1, D])
    ln2g_d = din("ln2_g", [1, D])
    ln2b_d = din("ln2_b", [1, D])

    out_d = nc.dram_tensor("out", [2, SEQ, D], F32, kind="ExternalOutput").ap()
    x1_d = dscr("x1_scr", [2, SEQ, D])
    winbf_d = dscr("winbf_scr", [8, 128, DIN], BF16)
    woutbf_d = dscr("woutbf_scr", [16, 128, D], BF16)
    modb_d = dscr("modb_scr", [128, 8 * D])
    ys5_d = dscr("ys5_scr", [2, 4, 128, SEQ], BF16)
    dbg_d = {}
    for name, shape in dbg:
        dbg_d[name] = nc.dram_tensor("dbg_" + name, list(shape), F32, kind="ExternalOutput").ap()

    out_chans = []

    with contextlib.ExitStack() as st:
        def sb(name, shape, dt=F32):
            return st.enter_context(nc.sbuf_tensor("t%d_%s" % (next(_uid), name), list(shape), dt))

        def V(fn, reads, writes):
            return S.add("dve", fn, reads, writes)

        def G(fn, reads, writes):
            return S.add("pool", fn, reads, writes)

        def A(fn, reads, writes):
            return S.add("act", fn, reads, writes)

        def T(fn, reads, writes):
            return S.add("pe", fn, reads, writes)

        def DMA(out, in_, reads, writes, chan=None):
            return S.dma("sp", lambda e, o=out, i=in_: e.dma_start(out=o, in_=i), reads, writes, chan=chan)

        def mm(out, lhsT, rhs, start, stop, reads, writes):
            return T(lambda e, o=out, l=lhsT, r=rhs, a=start, b=stop: e.matmul(o, lhsT=l, rhs=r, start=a, stop=b),
                     reads, writes)

        def tt(eng, out, in0, in1, op, reads, writes):
            return S.add(eng, lambda e, o=out, a=in0, b=in1, p=op: e.tensor_tensor(out=o, in0=a, in1=b, op=p),
                         reads, writes)

        def ts(eng, out, in0, s1, s2, op0, op1, reads, writes):
            if op1 is None:
                return S.add(eng, lambda e, o=out, a=in0, x=s1, p=op0: e.tensor_scalar(out=o, in0=a, scalar1=x, scalar2=None, op0=p),
                             reads, writes)
            return S.add(eng, lambda e, o=out, a=in0, x=s1, y=s2, p=op0, q=op1: e.tensor_scalar(out=o, in0=a, scalar1=x, scalar2=y, op0=p, op1=q),
                         reads, writes)

        def stt(out, in0, scalar, in1, op0, op1, reads, writes):
            return V(lambda e, o=out, a=in0, s=scalar, b=in1, p=op0, q=op1: e.scalar_tensor_tensor(out=o, in0=a, scalar=s, in1=b, op0=p, op1=q),
                     reads, writes)

        def act(out, in_, func, reads, writes, bias=None, scale=None):
            kw = {}
            if bias is not None:
                kw["bias"] = bias
            if scale is not None:
                kw["scale"] = scale
            return A(lambda e, o=out, i=in_, f=func, k=kw: e.activation(out=o, in_=i, func=f, **k), reads, writes)

        def cp(eng, out, in_, reads, writes):
            if eng == "act":
                return A(lambda e, o=out, i=in_: e.copy(out=o, in_=i), reads, writes)
            return S.add(eng, lambda e, o=out, i=in_: e.tensor_copy(out=o, in_=i), reads, writes)

        def memset(eng, ap, val, writes):
            return S.add(eng, lambda e, a=ap, v=val: e.memset(a, v), (), writes)

        PS = [st.enter_context(nc.psum_tensor("ps%d" % i, [128, 512], F32)) for i in range(8)]
        PSB = [p[:].bitcast(BF16) for p in PS]

        def pk(i):
            return "ps%d" % i

        ident_bf = sb("ident_bf", [128, 128], BF16)
        ident_f = sb("ident_f", [128, 128])
        triU = sb("triU", [128, 128])
        strictL = sb("strictL", [128, 128])
        ones_f = sb("ones_f", [128, 128])
        ones_bf = sb("ones_bf", [1, 128], BF16)
        bar_t = sb("bar_t", [128, 16])
        bar_d = sb("bar_d", [1, 4])

        def mk_bar(E, tag):
            col = {"act": 0, "dve": 2, "pool": 4}.get(E, 6) + tag
            if E == "pe":
                return lambda e: e.matmul(PS[7][0:1, 0:1], lhsT=ones_bf[0:1, 0:1], rhs=ones_bf[0:1, 0:1], start=True, stop=True)
            if E == "sp":
                return lambda e: e.dma_start(out=bar_d[0:1, 0:4], in_=b_ada_d[0:1, 0:4])
            if E == "act":
                return lambda e, c=col: e.copy(out=bar_t[:, c:c + 1], in_=ones_f[:, 0:1])
            return lambda e, c=col: e.memset(bar_t[:, c:c + 1], 0.0)

        memset("pool", ident_f[:], 0.0, ["ident_f"])
        G(lambda e: e.affine_select(out=ident_f[:], in_=ident_f[:], pattern=[[-1, 128]], compare_op=ALU.not_equal,
                                    fill=1.0, base=0, channel_multiplier=1), ["ident_f"], ["ident_f"])
        cp("dve", ident_bf[:], ident_f[:], ["ident_f"], ["ident_bf"])
        memset("pool", ones_f[:], 1.0, ["ones_f"])
        memset("pool", ones_bf[:], 1.0, ["ones_bf"])
        G(lambda e: e.affine_select(out=triU[:], in_=ones_f[:], pattern=[[1, 128]], compare_op=ALU.is_ge,
                                    fill=0.0, base=0, channel_multiplier=-1), ["ones_f"], ["triU"])
        G(lambda e: e.affine_select(out=strictL[:], in_=ones_f[:], pattern=[[-1, 128]], compare_op=ALU.is_gt,
                                    fill=0.0, base=0, channel_multiplier=1), ["ones_f"], ["strictL"])

        def load_small(alloc, name, src, shape, dt=F32):
            t = alloc(name, shape, dt)
            DMA(t[:], src, [], [name])
            return t

        mod1T = sb("mod1T", [128, 16, 2])
        b_gluT = load_small(sb, "b_gluT", b_gluT_d[:, :], [128, 4])

        with contextlib.ExitStack() as stA1:
            def sa1(name, shape, dt=F32):
                return stA1.enter_context(nc.sbuf_tensor("t%d_%s" % (next(_uid), name), list(shape), dt))

            Tg = sa1("Tg", [128, 32, 128], BF16)
            Wre = sa1("Wre", [128, 32, 128], BF16)
            Wim = sa1("Wim", [128, 32, 128], BF16)
            Gre = sa1("Gre", [128, 16, 128], BF16)
            Gim = sa1("Gim", [128, 16, 128], BF16)
            ctab = sa1("ctab", [128, 16, 64])
            stab = sa1("stab", [128, 16, 64])
            r8 = sa1("r8", [128, 16])
            rho8tab = sa1("rho8tab", [128, 16, 64])
            wglu_bf = sa1("wglu_bf", [128, 4, 512], BF16)
            ws5 = sa1("ws5", [128, 8, 512], BF16)

            with contextlib.ExitStack() as st5:
                def s5t(name, shape, dt=F32):
                    return st5.enter_context(nc.sbuf_tensor("t%d_%s" % (next(_uid), name), list(shape), dt))

                b_adaT = load_small(s5t, "b_adaT", b_adaT_d[:, :], [128, 16])
                condT = s5t("condT", [128, 8, 2])
                DMA(condT[:], cT_d[:, :, :], [], ["condT"])
                act(condT[:], condT[:], AF.Silu, ["condT"], ["condT"])
                condrep = s5t("condrep", [128, 8, 2, 128])
                for b in range(2):
                    cp("dve", condrep[:, :, b, :], condT[:, :, b:b + 1].to_broadcast([128, 8, 128]), ["condT"], ["condrep"])
                modb = s5t("modb", [128, 2, 4, D])
                badab = s5t("badab", [128, 4 * D])
                DMA(badab[:], b_ada_d[0:1, 2 * D:6 * D].partition_broadcast(128), [], ["badab"])
                wa = [s5t("wa%d" % i, [128, 8, 512]) for i in range(2)]
                for blk in range(12):
                    wt = wa[blk % 2]
                    wn = "wa%d" % (blk % 2)
                    DMA(wt[:], w_ada_d[:, blk * 512:(blk + 1) * 512].rearrange("(k p) n -> p k n", p=128), [], [wn])
                    if blk < 4:
                        for jt in range(4):
                            j = blk * 4 + jt
                            bank = 4 + (j % 2)
                            for k in range(8):
                                mm(PS[bank][:, 0:2], wt[:, k, jt * 128:(jt + 1) * 128], condT[:, k, :], k == 0, k == 7,
                                   [wn, "condT"], [pk(bank)])
                            ts("dve", mod1T[:, j, :], PS[bank][:, 0:2], b_adaT[:, j:j + 1], None, ALU.add, None,
                               [pk(bank), "b_adaT"], ["mod1T"])
                    else:
                        q = (blk - 4) // 2
                        half = (blk - 4) % 2
                        for b in range(2):
                            bank = 6 + b
                            for k in range(8):
                                mm(PS[bank][:, :], condrep[:, k, b, :], wt[:, k, :], k == 0, k == 7, [wn, "condrep"], [pk(bank)])
                            tt("dve", modb[:, b, q, half * 512:(half + 1) * 512], PS[bank][:, :],
                               badab[:, q * D + half * 512: q * D + (half + 1) * 512], ALU.add, [pk(bank), "badab"], ["modb"])
                ts("dve", mod1T[:, 8:16, :], mod1T[:, 8:16, :], 1.0, None, ALU.add, None, ["mod1T"], ["mod1T"])
                for b in range(2):
                    for q in (0, 2, 3):
                        ts("dve", modb[:, b, q, :], modb[:, b, q, :], 1.0, None, ALU.add, None, ["modb"], ["modb"])
                DMA(modb_d[:, :], modb[:].rearrange("p b q d -> p (b q d)"), ["modb"], ["modb_scr"], chan=("modw",))
                S.barrier(mk_bar, ["modb_scr"])

            with contextlib.ExitStack() as st5:
                def s5t(name, shape, dt=F32):
                    return st5.enter_context(nc.sbuf_tensor("t%d_%s" % (next(_uid), name), list(shape), dt))

                normwT = load_small(s5t, "normwT", normwT_d[:, :], [128, 12])
                NSL = 4
                wl = [s5t("wb%d" % i, [128, 8, 512]) for i in range(NSL)]
                wc = [s5t("wc%d" % i, [128, 4096], BF16) for i in range(NSL)]
                n_ = 0
                for k in range(8):
                    for (c0, c1) in ((0, 2316), (2316, DIN)):
                        ln_, cn_ = "wb%d" % (n_ % NSL), "wc%d" % (n_ % NSL)
                        w_ = c1 - c0
                        wlf = wl[n_ % NSL][:].rearrange("p a b -> p (a b)")
                        DMA(wlf[:, 0:w_], w_in_d[k * 128:(k + 1) * 128, c0:c1], [], [ln_])
                        cp("dve", wc[n_ % NSL][:, 0:1158], wlf[:, 0:1158], [ln_], [cn_])
                        cp("dve", wc[n_ % NSL][:, 1158:w_], wlf[:, 1158:w_], [ln_], [cn_])
                        DMA(winbf_d[k, :, c0:c1], wc[n_ % NSL][:, 0:w_], [cn_], ["winbf_scr"], chan=("wscr", n_ % NSL))
                        if c1 == DIN:
                            cp("act", ws5[:, k, :], wlf[:, 4120 - 2316:4632 - 2316], [ln_], ["ws5"])
                        n_ += 1
                for kt in range(16):
                    ln_, cn_ = "wb%d" % (n_ % NSL), "wc%d" % (n_ % NSL)
                    wlf = wl[n_ % NSL][:].rearrange("p a b -> p (a b)")
                    DMA(wlf[:, 0:D], w_out_d[kt * 128:(kt + 1) * 128, :], [], [ln_])
                    if kt < 12:
                        ts("dve", wc[n_ % NSL][:, 0:D], wlf[:, 0:D], normwT[:, kt:kt + 1], None, ALU.mult, None,
                           [ln_, "normwT"], [cn_])
                    else:
                        cp("dve" if kt % 2 else "act", wc[n_ % NSL][:, 0:D], wlf[:, 0:D], [ln_], [cn_])
                    DMA(woutbf_d[kt], wc[n_ % NSL][:, 0:D], [cn_], ["woutbf_scr"], chan=("wscr", n_ % NSL))
                    n_ += 1
                for kt in range(4):
                    ln_ = "wb%d" % (n_ % NSL)
                    wlf = wl[n_ % NSL][:].rearrange("p a b -> p (a b)")
                    DMA(wlf[:, 0:512], w_glu_d[kt * 128:(kt + 1) * 128, :], [], [ln_])
                    cp("dve", wglu_bf[:, kt, :], wlf[:, 0:512], [ln_], ["wglu_bf"])
                    n_ += 1

                S.barrier(mk_bar, ["winbf_scr", "woutbf_scr"])

            with contextlib.ExitStack() as st5:
                def s5t(name, shape, dt=F32):
                    return st5.enter_context(nc.sbuf_tensor("t%d_%s" % (next(_uid), name), list(shape), dt))

                ar = s5t("s5_ar", [128, 16]); ai = s5t("s5_ai", [128, 16]); dtg = s5t("s5_dt", [128, 16])
                DMA(ar[:], s5ar_d[:, :], [], ["s5_ar"]); DMA(ai[:], s5ai_d[:, :], [], ["s5_ai"]); DMA(dtg[:], s5ldt_d[:, :], [], ["s5_dt"])
                br = s5t("s5_br", [128, 16, 16]); bi = s5t("s5_bi", [128, 16, 16])
                cr = s5t("s5_cr", [128, 16, 16]); ci = s5t("s5_ci", [128, 16, 16])
                dv = s5t("s5_dv", [128, 32])
                DMA(br[:], s5br_d[:, :, :], [], ["s5_br"]); DMA(bi[:], s5bi_d[:, :, :], [], ["s5_bi"])
                DMA(cr[:], s5cr_d[:, :, :], [], ["s5_cr"]); DMA(ci[:], s5ci_d[:, :, :], [], ["s5_ci"])
                DMA(dv[:], s5dv_d[:, :], [], ["s5_dv"])
                act(dtg[:], dtg[:], AF.Exp, ["s5_dt"], ["s5_dt"])
                rho = s5t("s5_rho", [128, 16]); th = s5t("s5_th", [128, 16])
                tmp = [s5t("s5_tmp%d" % i, [128, 16]) for i in range(6)]
                tmpi = s5t("s5_tmpi", [128, 16], I32)
                tt("dve", rho[:], ar[:], dtg[:], ALU.mult, ["s5_ar", "s5_dt"], ["s5_rho"])
                act(rho[:], rho[:], AF.Exp, ["s5_rho"], ["s5_rho"])
                tt("dve", th[:], ai[:], dtg[:], ALU.mult, ["s5_ai", "s5_dt"], ["s5_th"])

                def sin_of(out, src, shift, rs, ws):
                    ts("dve", tmp[0][:], src, shift, 1.0 / TWO_PI, ALU.add, ALU.mult, rs, ["s5_tmp0"])
                    cp("dve", tmpi[:], tmp[0][:], ["s5_tmp0"], ["s5_tmpi"])
                    cp("dve", tmp[1][:], tmpi[:], ["s5_tmpi"], ["s5_tmp1"])
                    tt("dve", tmp[0][:], tmp[0][:], tmp[1][:], ALU.subtract, ["s5_tmp0", "s5_tmp1"], ["s5_tmp0"])
                    ts("dve", tmp[0][:], tmp[0][:], TWO_PI, 3.14159, ALU.mult, ALU.min, ["s5_tmp0"], ["s5_tmp0"])
                    ts("dve", tmp[0][:], tmp[0][:], -3.14159, None, ALU.max, None, ["s5_tmp0"], ["s5_tmp0"])
                    act(out, tmp[0][:], AF.Sin, ["s5_tmp0"], ws)

                cs = s5t("s5_cs", [128, 16]); sn = s5t("s5_sn", [128, 16])
                sin_of(sn[:], th[:], 0.0, ["s5_th"], ["s5_sn"])
                sin_of(cs[:], th[:], math.pi / 2.0, ["s5_th"], ["s5_cs"])
                abr = s5t("s5_abr", [128, 16]); abi = s5t("s5_abi", [128, 16])
                tt("dve", abr[:], rho[:], cs[:], ALU.mult, ["s5_rho", "s5_cs"], ["s5_abr"])
                tt("dve", abi[:], rho[:], sn[:], ALU.mult, ["s5_rho", "s5_sn"], ["s5_abi"])
                den = tmp[2]; nre = tmp[3]; cfr = s5t("s5_cfr", [128, 16]); cfi = s5t("s5_cfi", [128, 16])
                tt("dve", den[:], ar[:], ar[:], ALU.mult, ["s5_ar"], ["s5_tmp2"])
                tt("dve", tmp[4][:], ai[:], ai[:], ALU.mult, ["s5_ai"], ["s5_tmp4"])
                tt("dve", den[:], den[:], tmp[4][:], ALU.add, ["s5_tmp2", "s5_tmp4"], ["s5_tmp2"])
                V(lambda e: e.reciprocal(out=den[:], in_=den[:]), ["s5_tmp2"], ["s5_tmp2"])
                ts("dve", nre[:], abr[:], -1.0, None, ALU.add, None, ["s5_abr"], ["s5_tmp3"])
                tt("dve", cfr[:], nre[:], ar[:], ALU.mult, ["s5_tmp3", "s5_ar"], ["s5_cfr"])
                tt("dve", tmp[4][:], abi[:], ai[:], ALU.mult, ["s5_abi", "s5_ai"], ["s5_tmp4"])
                tt("dve", cfr[:], cfr[:], tmp[4][:], ALU.add, ["s5_cfr", "s5_tmp4"], ["s5_cfr"])
                tt("dve", cfr[:], cfr[:], den[:], ALU.mult, ["s5_cfr", "s5_tmp2"], ["s5_cfr"])
                tt("dve", cfi[:], abi[:], ar[:], ALU.mult, ["s5_abi", "s5_ar"], ["s5_cfi"])
                tt("dve", tmp[4][:], nre[:], ai[:], ALU.mult, ["s5_tmp3", "s5_ai"], ["s5_tmp4"])
                tt("dve", cfi[:], cfi[:], tmp[4][:], ALU.subtract, ["s5_cfi", "s5_tmp4"], ["s5_cfi"])
                tt("dve", cfi[:], cfi[:], den[:], ALU.mult, ["s5_cfi", "s5_tmp2"], ["s5_cfi"])
                Bbr = s5t("s5_Bbr", [128, 16, 16]); Bbi = s5t("s5_Bbi", [128, 16, 16])
                t3a = s5t("s5_t3a", [128, 16, 16])

                def bc16(a):
                    return a[:].unsqueeze(2).to_broadcast([128, 16, 16])

                tt("dve", Bbr[:], br[:], bc16(cfr), ALU.mult, ["s5_br", "s5_cfr"], ["s5_Bbr"])
                tt("dve", t3a[:], bi[:], bc16(cfi), ALU.mult, ["s5_bi", "s5_cfi"], ["s5_t3a"])
                tt("dve", Bbr[:], Bbr[:], t3a[:], ALU.subtract, ["s5_Bbr", "s5_t3a"], ["s5_Bbr"])
                tt("dve", Bbi[:], bi[:], bc16(cfr), ALU.mult, ["s5_bi", "s5_cfr"], ["s5_Bbi"])
                tt("dve", t3a[:], br[:], bc16(cfi), ALU.mult, ["s5_br", "s5_cfi"], ["s5_t3a"])
                tt("dve", Bbi[:], Bbi[:], t3a[:], ALU.add, ["s5_Bbi", "s5_t3a"], ["s5_Bbi"])
                Pr = s5t("s5_Pr", [128, 16, 9]); Pi = s5t("s5_Pi", [128, 16, 9])
                memset("dve", Pr[:, :, 0:1], 1.0, ["s5_Pr"]); memset("dve", Pi[:, :, 0:1], 0.0, ["s5_Pi"])

                def cmul_small(outr, outi, ar_, ai_, br_, bi_, rs, ws):
                    tt("dve", tmp[4][:], ar_, br_, ALU.mult, rs, ["s5_tmp4"])
                    tt("dve", tmp[5][:], ai_, bi_, ALU.mult, rs, ["s5_tmp5"])
                    tt("dve", tmp[0][:], ar_, bi_, ALU.mult, rs, ["s5_tmp0"])
                    tt("dve", tmp[1][:], ai_, br_, ALU.mult, rs, ["s5_tmp1"])
                    tt("dve", outr, tmp[4][:], tmp[5][:], ALU.subtract, ["s5_tmp4", "s5_tmp5"], ws)
                    tt("dve", outi, tmp[0][:], tmp[1][:], ALU.add, ["s5_tmp0", "s5_tmp1"], ws)

                for k in range(1, 9):
                    cmul_small(Pr[:, :, k], Pi[:, :, k], Pr[:, :, k - 1], Pi[:, :, k - 1], abr[:], abi[:],
                               ["s5_Pr", "s5_Pi", "s5_abr", "s5_abi"], ["s5_Pr", "s5_Pi"])
                Qre = s5t("s5_Qre", [128, 16, 16, 16]); Qie = s5t("s5_Qie", [128, 16, 16, 16])
                t4a = s5t("s5_t4a", [128, 16, 9, 16])
                memset("pool", Qre[:, :, 0:7, :], 0.0, ["s5_Qre"]); memset("pool", Qie[:, :, 0:7, :], 0.0, ["s5_Qie"])

                def cb(a):
                    return a[:].unsqueeze(2).to_broadcast([128, 16, 9, 16])

                def pb(a):
                    return a[:].unsqueeze(3).to_broadcast([128, 16, 9, 16])

                tt("dve", Qre[:, :, 7:16, :], cb(cr), pb(Pr), ALU.mult, ["s5_cr", "s5_Pr"], ["s5_Qre"])
                tt("dve", t4a[:], cb(ci), pb(Pi), ALU.mult, ["s5_ci", "s5_Pi"], ["s5_t4a"])
                tt("dve", Qre[:, :, 7:16, :], Qre[:, :, 7:16, :], t4a[:], ALU.subtract, ["s5_Qre", "s5_t4a"], ["s5_Qre"])
                tt("dve", Qie[:, :, 7:16, :], cb(cr), pb(Pi), ALU.mult, ["s5_cr", "s5_Pi"], ["s5_Qie"])
                tt("dve", t4a[:], cb(ci), pb(Pr), ALU.mult, ["s5_ci", "s5_Pr"], ["s5_t4a"])
                tt("dve", Qie[:, :, 7:16, :], Qie[:, :, 7:16, :], t4a[:], ALU.add, ["s5_Qie", "s5_t4a"], ["s5_Qie"])
                ts("dve", Qie[:, :, 7:16, :], Qie[:, :, 7:16, :], -1.0, None, ALU.mult, None, ["s5_Qie"], ["s5_Qie"])
                cp("dve", Gre[:].rearrange("p a (j h) -> p a j h", h=16), Qre[:, :, 8:16, :], ["s5_Qre"], ["Gre"])
                cp("dve", Gim[:].rearrange("p a (j h) -> p a j h", h=16), Qie[:, :, 8:16, :], ["s5_Qie"], ["Gim"])
                WTr = s5t("s5_WTr", [128, 16, 8, 16]); WTi = s5t("s5_WTi", [128, 16, 8, 16])
                for i in range(8):
                    k = 7 - i
                    pr_b = Pr[:, :, k:k + 1].to_broadcast([128, 16, 16])
                    pi_b = Pi[:, :, k:k + 1].to_broadcast([128, 16, 16])
                    tt("dve", WTr[:, :, i, :], Bbr[:], pr_b, ALU.mult, ["s5_Bbr", "s5_Pr"], ["s5_WTr"])
                    tt("dve", t3a[:], Bbi[:], pi_b, ALU.mult, ["s5_Bbi", "s5_Pi"], ["s5_t3a"])
                    tt("dve", WTr[:, :, i, :], WTr[:, :, i, :], t3a[:], ALU.subtract, ["s5_WTr", "s5_t3a"], ["s5_WTr"])
                    tt("dve", WTi[:, :, i, :], Bbi[:], pr_b, ALU.mult, ["s5_Bbi", "s5_Pr"], ["s5_WTi"])
                    tt("dve", t3a[:], Bbr[:], pi_b, ALU.mult, ["s5_Bbr", "s5_Pi"], ["s5_t3a"])
                    tt("dve", WTi[:, :, i, :], WTi[:, :, i, :], t3a[:], ALU.add, ["s5_WTi", "s5_t3a"], ["s5_WTi"])
                memset("pool", Wre[:], 0.0, ["Wre"]); memset("pool", Wim[:], 0.0, ["Wim"])
                for pr_ in range(16):
                    for (src, dst, sname, dname, bank) in ((WTr, Wre, "s5_WTr", "Wre", 0), (WTi, Wim, "s5_WTi", "Wim", 1)):
                        T(lambda e, o=PS[bank][:, 0:128], i_=src[:, pr_, :, :].rearrange("p i h -> p (i h)"): e.transpose(out=o, in_=i_, identity=ident_f[:]),
                          [sname, "ident_f"], [pk(bank)])
                        cp("dve", dst[:, 2 * pr_, 0:64], PS[bank][:, 0:64], [pk(bank)], [dname])
                        cp("act", dst[:, 2 * pr_ + 1, 64:128], PS[bank][:, 64:128], [pk(bank)], [dname])
                BbZr = s5t("s5_BbZr", [128, 16, 15, 16]); BbZi = s5t("s5_BbZi", [128, 16, 15, 16])
                memset("pool", BbZr[:], 0.0, ["s5_BbZr"]); memset("pool", BbZi[:], 0.0, ["s5_BbZi"])
                cp("dve", BbZr[:, :, 7, :], Bbr[:], ["s5_Bbr"], ["s5_BbZr"])
                cp("dve", BbZi[:, :, 7, :], Bbi[:], ["s5_Bbi"], ["s5_BbZi"])
                BbZr_b = s5t("s5_BbZr_b", [128, 16, 15, 16], BF16); BbZi_b = s5t("s5_BbZi_b", [128, 16, 15, 16], BF16)
                Qre_b = s5t("s5_Qre_b", [128, 16, 16, 16], BF16); Qie_b = s5t("s5_Qie_b", [128, 16, 16, 16], BF16)
                cp("dve", BbZr_b[:], BbZr[:], ["s5_BbZr"], ["s5_BbZr_b"]); cp("act", BbZi_b[:], BbZi[:], ["s5_BbZi"], ["s5_BbZi_b"])
                cp("dve", Qre_b[:], Qre[:], ["s5_Qre"], ["s5_Qre_b"]); cp("act", Qie_b[:], Qie[:], ["s5_Qie"], ["s5_Qie_b"])
                for g in range(32):
                    pr_, g2 = g // 2, g % 2
                    bank = 2 + (g % 2)
                    lo, hi = 64 * g2, 64 * g2 + 64
                    n = 0
                    for i in range(8):
                        for (bz, qq, bzn, qn) in ((BbZr_b, Qre_b, "s5_BbZr_b", "s5_Qre_b"), (BbZi_b, Qie_b, "s5_BbZi_b", "s5_Qie_b")):
                            lhs = bz[lo:hi, pr_, 7 - i:7 - i + 8, :].rearrange("p a h -> p (a h)")
                            rhs = qq[lo:hi, pr_, 7 - i:7 - i + 8, :].rearrange("p a h -> p (a h)")
                            mm(PS[bank][:, 0:128], lhs, rhs, n == 0, n == 15, [bzn, qn], [pk(bank)])
                            n += 1
                    stt(Tg[:, g, :], ident_f[:], dv[:, g:g + 1], PS[bank][:, 0:128], ALU.mult, ALU.add,
                        ["ident_f", "s5_dv", pk(bank)], ["Tg"])
                e8r = s5t("s5_e8r", [128, 16]); e8i = s5t("s5_e8i", [128, 16])
                cp("dve", e8r[:], cs[:], ["s5_cs"], ["s5_e8r"]); cp("dve", e8i[:], sn[:], ["s5_sn"], ["s5_e8i"])
                cp("dve", r8[:], rho[:], ["s5_rho"], ["r8"])
                e2r = s5t("s5_e2r", [128, 16]); e2i = s5t("s5_e2i", [128, 16])
                for _ in range(3):
                    cmul_small(e2r[:], e2i[:], e8r[:], e8i[:], e8r[:], e8i[:], ["s5_e8r", "s5_e8i"], ["s5_e2r", "s5_e2i"])
                    cp("dve", e8r[:], e2r[:], ["s5_e2r"], ["s5_e8r"]); cp("dve", e8i[:], e2i[:], ["s5_e2i"], ["s5_e8i"])
                    tt("dve", r8[:], r8[:], r8[:], ALU.mult, ["r8"], ["r8"])
                cp("dve", rho8tab[:], r8[:].unsqueeze(2).to_broadcast([128, 16, 64]), ["r8"], ["rho8tab"])
                memset("dve", rho8tab[:, :, 0:1], 0.0, ["rho8tab"])
                cp("dve", ctab[:, :, 0], e8r[:], ["s5_e8r"], ["ctab"]); cp("dve", stab[:, :, 0], e8i[:], ["s5_e8i"], ["stab"])
                tb = [s5t("s5_tb%d" % i, [128, 16, 32]) for i in range(4)]
                m = 1
                while m < 64:
                    er = e8r[:].unsqueeze(2).to_broadcast([128, 16, m]); ei = e8i[:].unsqueeze(2).to_broadcast([128, 16, m])
                    tt("dve", tb[0][:, :, 0:m], ctab[:, :, 0:m], er, ALU.mult, ["ctab", "s5_e8r"], ["s5_tb0"])
                    tt("dve", tb[1][:, :, 0:m], stab[:, :, 0:m], ei, ALU.mult, ["stab", "s5_e8i"], ["s5_tb1"])
                    tt("dve", tb[2][:, :, 0:m], ctab[:, :, 0:m], ei, ALU.mult, ["ctab", "s5_e8i"], ["s5_tb2"])
                    tt("dve", tb[3][:, :, 0:m], stab[:, :, 0:m], er, ALU.mult, ["stab", "s5_e8r"], ["s5_tb3"])
                    tt("dve", ctab[:, :, m:2 * m], tb[0][:, :, 0:m], tb[1][:, :, 0:m], ALU.subtract, ["s5_tb0", "s5_tb1"], ["ctab"])
                    tt("dve", stab[:, :, m:2 * m], tb[2][:, :, 0:m], tb[3][:, :, 0:m], ALU.add, ["s5_tb2", "s5_tb3"], ["stab"])
                    cmul_small(e2r[:], e2i[:], e8r[:], e8i[:], e8r[:], e8i[:], ["s5_e8r", "s5_e8i"], ["s5_e2r", "s5_e2i"])
                    cp("dve", e8r[:], e2r[:], ["s5_e2r"], ["s5_e8r"]); cp("dve", e8i[:], e2i[:], ["s5_e2i"], ["s5_e8i"])
                    m *= 2
                if "Tg" in dbg_d:
                    tgf = s5t("dbg_tgf", [128, 32 * 128])
                    cp("dve", tgf[:], Tg[:].rearrange("p g c -> p (g c)"), ["Tg"], ["dbg_tgf"])
                    DMA(dbg_d["Tg"][:, :], tgf[:], ["dbg_tgf"], [], chan=("dbg", "Tg")); out_chans.append(("dbg", "Tg"))
                if "ctab" in dbg_d:
                    DMA(dbg_d["ctab"][:, :], ctab[:].rearrange("p a b -> p (a b)"), ["ctab"], [], chan=("dbg", "ctab")); out_chans.append(("dbg", "ctab"))
                S.barrier(mk_bar, [])

            if "A1" in phases:
                xTin = [sa1("xTin%d" % i, [128, 512]) for i in range(2)]
                u1T_ = [sa1("u1T%d" % i, [128, 8, 512], BF16) for i in range(2)]
                UT8g_ = [sa1("UT8g%d" % i, [64, 32, 8, 16], BF16) for i in range(2)]
                Ug = sa1("Ug", [128, 32, 64], BF16)
                s5w = [sa1("s5w%d" % i, [128, 16, 64]) for i in range(6)]
                Sin_re = sa1("Sin_re", [128, 16]); Sin_im = sa1("Sin_im", [128, 16])
                fixr = sa1("fixr", [128, 16]); fixi = sa1("fixi", [128, 16])
                SfBr = sa1("SfBr", [128, 16, 65], BF16); SfBi = sa1("SfBi", [128, 16, 65], BF16)
                Ygt = sa1("Ygt", [64, 8, 512], BF16)
                y5a = sa1("y5a", [64, 1024])
                ygT = sa1("ygT", [128, 4, 512], BF16)
                gsig = sa1("gsig", [128, 512])
                yo5 = sa1("yo5", [128, 4, 512], BF16)
                def P_prep(si):
                    b, sbi = si // NSB, si % NSB
                    t0 = sbi * 512
                    pa = si % 2
                    for k in range(8):
                        xn = "xTin%d" % (k % 2)
                        DMA(xTin[k % 2][:], xT_d[b, k * 128:(k + 1) * 128, t0:t0 + 512], [], [xn])
                        ts("dve", u1T_[pa][:, k, :], xTin[k % 2][:], mod1T[:, 8 + k, b:b + 1], mod1T[:, k, b:b + 1],
                           ALU.mult, ALU.add, [xn, "mod1T"], ["u1T%d" % pa])

                def P_mm(si):
                    pa = si % 2
                    for i in range(8):
                        bank = 6 + (i % 2)
                        for k in range(8):
                            mm(PS[bank][0:64, :], u1T_[pa][:, k, i::8], ws5[:, k, :], k == 0, k == 7, ["u1T%d" % pa, "ws5"], [pk(bank)])
                        cp("act", UT8g_[pa][:, :, i, :], PS[bank][0:64, :].rearrange("p (g h) -> p g h", h=16),
                           [pk(bank)], ["UT8g%d" % pa])

                P_prep(0)
                P_mm(0)
                for si in range(2 * NSB):
                    if True:
                        b, sbi = si // NSB, si % NSB
                        t0 = sbi * 512
                        pa = si % 2
                        UT8g = UT8g_[pa]
                        if sbi == 0:
                            memset("pool", Sin_re[:], 0.0, ["Sin_re"]); memset("pool", Sin_im[:], 0.0, ["Sin_im"])
                        if si + 1 < 2 * NSB:
                            P_prep(si + 1)
                        for g in range(32):
                            bank = g // 16
                            T(lambda e, o=PSB[bank][:, (g % 16) * 64:(g % 16) * 64 + 64], i_=UT8g[:, g, :, :].rearrange("p i h -> p (i h)"):
                              e.transpose(out=o, in_=i_, identity=ident_bf[0:64, 0:64]), ["UT8g%d" % pa, "ident_bf"], [pk(bank)])
                        cp("dve", Ug[:, 0:16, :], PSB[0][:, :].rearrange("p (g b) -> p g b", b=64), [pk(0)], ["Ug"])
                        cp("act", Ug[:, 16:32, :], PSB[1][:, :].rearrange("p (g b) -> p g b", b=64), [pk(1)], ["Ug"])
                        for (Wx, wname, b0) in ((Wre, "Wre", 0), (Wim, "Wim", 2)):
                            for pr_ in range(16):
                                bank = b0 + pr_ // 8
                                o = PS[bank][:, (pr_ % 8) * 64:(pr_ % 8) * 64 + 64]
                                mm(o, Wx[:, 2 * pr_, :], Ug[:, 2 * pr_, :], True, False, [wname, "Ug"], [pk(bank)])
                                mm(o, Wx[:, 2 * pr_ + 1, :], Ug[:, 2 * pr_ + 1, :], False, True, [wname, "Ug"], [pk(bank)])
                        if si + 1 < 2 * NSB:
                            P_mm(si + 1)
                        for h_ in range(2):
                            Er = PS[0 + h_][:, :].rearrange("p (a b) -> p a b", b=64)
                            Ei = PS[2 + h_][:, :].rearrange("p (a b) -> p a b", b=64)
                            sl = slice(h_ * 8, h_ * 8 + 8)
                            tt("dve", s5w[0][:, sl, :], Er, ctab[:, sl, :], ALU.mult, [pk(h_), "ctab"], ["s5w0"])
                            tt("dve", s5w[1][:, sl, :], Ei, stab[:, sl, :], ALU.mult, [pk(2 + h_), "stab"], ["s5w1"])
                            tt("dve", s5w[2][:, sl, :], Ei, ctab[:, sl, :], ALU.mult, [pk(2 + h_), "ctab"], ["s5w2"])
                            tt("dve", s5w[3][:, sl, :], Er, stab[:, sl, :], ALU.mult, [pk(h_), "stab"], ["s5w3"])
                        tt("dve", s5w[0][:], s5w[0][:], s5w[1][:], ALU.add, ["s5w0", "s5w1"], ["s5w0"])
                        tt("dve", s5w[2][:], s5w[2][:], s5w[3][:], ALU.subtract, ["s5w2", "s5w3"], ["s5w2"])
                        cp("dve", SfBr[:, :, 0], Sin_re[:], ["Sin_re"], ["SfBr"])
                        cp("dve", SfBi[:, :, 0], Sin_im[:], ["Sin_im"], ["SfBi"])
                        tt("dve", fixr[:], r8[:], Sin_re[:], ALU.mult, ["r8", "Sin_re"], ["fixr"])
                        tt("dve", s5w[0][:, :, 0], s5w[0][:, :, 0], fixr[:], ALU.add, ["s5w0", "fixr"], ["s5w0"])
                        tt("dve", fixi[:], r8[:], Sin_im[:], ALU.mult, ["r8", "Sin_im"], ["fixi"])
                        tt("dve", s5w[2][:, :, 0], s5w[2][:, :, 0], fixi[:], ALU.add, ["s5w2", "fixi"], ["s5w2"])
                        rtf = rho8tab[:].rearrange("p a b -> p (a b)")
                        V(lambda e, o=s5w[4][:].rearrange("p a b -> p (a b)"), d0=rtf, d1=s5w[0][:].rearrange("p a b -> p (a b)"):
                          e.tensor_tensor_scan(out=o, data0=d0, data1=d1, initial=0.0, op0=ALU.mult, op1=ALU.add),
                          ["rho8tab", "s5w0"], ["s5w4"])
                        V(lambda e, o=s5w[5][:].rearrange("p a b -> p (a b)"), d0=rtf, d1=s5w[2][:].rearrange("p a b -> p (a b)"):
                          e.tensor_tensor_scan(out=o, data0=d0, data1=d1, initial=0.0, op0=ALU.mult, op1=ALU.add),
                          ["rho8tab", "s5w2"], ["s5w5"])
                        tt("dve", s5w[0][:], s5w[4][:], ctab[:], ALU.mult, ["s5w4", "ctab"], ["s5w0"])
                        tt("dve", s5w[1][:], s5w[5][:], stab[:], ALU.mult, ["s5w5", "stab"], ["s5w1"])
                        tt("dve", s5w[2][:], s5w[5][:], ctab[:], ALU.mult, ["s5w5", "ctab"], ["s5w2"])
                        tt("dve", s5w[3][:], s5w[4][:], stab[:], ALU.mult, ["s5w4", "stab"], ["s5w3"])
                        tt("dve", s5w[0][:], s5w[0][:], s5w[1][:], ALU.subtract, ["s5w0", "s5w1"], ["s5w0"])
                        tt("dve", s5w[2][:], s5w[2][:], s5w[3][:], ALU.add, ["s5w2", "s5w3"], ["s5w2"])
                        cp("dve", SfBr[:, :, 1:65], s5w[0][:], ["s5w0"], ["SfBr"])
                        cp("dve", SfBi[:, :, 1:65], s5w[2][:], ["s5w2"], ["SfBi"])
                        cp("dve", Sin_re[:], s5w[0][:, :, 63], ["s5w0"], ["Sin_re"])
                        cp("dve", Sin_im[:], s5w[2][:, :, 63], ["s5w2"], ["Sin_im"])
                        for q in range(4):
                            for gg in range(8):
                                g = q * 8 + gg
                                pr_, g2 = g // 2, g % 2
                                lo, hi = 64 * g2, 64 * g2 + 64
                                bank = gg // 4
                                o = PS[bank][0:64, (gg % 4) * 128:(gg % 4) * 128 + 128]
                                mm(o, Ug[:, g, :], Tg[:, g, :], True, False, ["Ug", "Tg"], [pk(bank)])
                                mm(o, SfBr[lo:hi, pr_, 0:64], Gre[lo:hi, pr_, :], False, False, ["SfBr", "Gre"], [pk(bank)])
                                mm(o, SfBi[lo:hi, pr_, 0:64], Gim[lo:hi, pr_, :], False, True, ["SfBi", "Gim"], [pk(bank)])
                            for hb in range(2):
                                yp = PS[hb][0:64, :]
                                o = Ygt[:, :, q * 128 + hb * 64: q * 128 + hb * 64 + 64].rearrange("p j (g h) -> p g j h", h=16)
                                act(o, yp.rearrange("p (g j h) -> p g j h", j=8, h=16), AF.Gelu_apprx_tanh, [pk(hb)], ["Ygt"])
                        for q in range(4):
                            for j in range(8):
                                bank = 2 + (j // 4) % 2
                                T(lambda e, o=PSB[bank][:, (j % 4) * 64:(j % 4) * 64 + 64], i_=Ygt[:, j, q * 128:(q + 1) * 128]:
                                  e.transpose(out=o, in_=i_, identity=ident_bf[0:64, 0:64]), ["Ygt", "ident_bf"], [pk(bank)])
                                if j % 4 == 3:
                                    jj = j // 4
                                    cp("act" if jj else "dve",
                                       ygT[:, q, :].rearrange("p (b j) -> p j b", j=8)[:, jj * 4:jj * 4 + 4, :],
                                       PSB[bank][:, 0:256].rearrange("p (j b) -> p j b", b=64), [pk(bank)], ["ygT"])
                        for m_ in range(4):
                            bank = 4 + (m_ % 2)
                            for q in range(4):
                                mm(PS[bank][:, :], wglu_bf[:, q, m_ * 128:(m_ + 1) * 128], ygT[:, q, :], q == 0, q == 3, ["wglu_bf", "ygT"], [pk(bank)])
                            act(gsig[:], PS[bank][:, :], AF.Sigmoid, [pk(bank), "b_gluT"], ["gsig"], bias=b_gluT[:, m_:m_ + 1])
                            tt("dve", yo5[:, m_, :], gsig[:], ygT[:, m_, :], ALU.mult, ["gsig", "ygT"], ["yo5"])
                        DMA(ys5_d[b, :, :, t0:t0 + 512].rearrange("q p t -> p q t"), yo5[:], ["yo5"], ["ys5_scr"], chan=("ys5w",))
                        if "ys5" in dbg_d and si == 0:
                            dbt = sa1("dbg_ys5", [128, 4, 512])
                            cp("dve", dbt[:], yo5[:], ["yo5"], ["dbg_ys5"])
                            DMA(dbg_d["ys5"][:, :], dbt[:].rearrange("p q t -> p (q t)"), ["dbg_ys5"], [], chan=("dbg", "ys5")); out_chans.append(("dbg", "ys5"))
            S.barrier(mk_bar, ["ys5_scr"])

        SBT = 256
        NCH = SBT // 128
        if "A2" in phases:
          with contextlib.ExitStack() as stA:
            def sa(name, shape, dt=F32):
                return stA.enter_context(nc.sbuf_tensor("t%d_%s" % (next(_uid), name), list(shape), dt))

            wout_bf = sa("wout_bf", [128, 16, D], BF16)
            DMA(wout_bf[:], woutbf_d[:, :, :].rearrange("k p n -> p k n"), ["woutbf_scr"], ["wout_bf"])
            convwT = load_small(sa, "convwT", convwT_d[:, :, :], [128, 20, 4])
            convbT = load_small(sa, "convbT", convbT_d[:, :], [128, 20])
            convb_bf = sa("convb_bf", [1, 2560], BF16)
            dtb_b = load_small(sa, "dtb_b", dtb_d[0:1, :].partition_broadcast(128), [128, NH])
            a_b = load_small(sa, "a_b", alog_d[0:1, :].partition_broadcast(128), [128, NH])
            act(a_b[:], a_b[:], AF.Exp, ["a_b"], ["a_b"])
            ts("dve", a_b[:], a_b[:], -1.0, None, ALU.mult, None, ["a_b"], ["a_b"])
            dfull = load_small(sa, "dfull", dssd_d[0:1, :].partition_broadcast(128), [128, NH])
            ln1g_b = load_small(sa, "ln1g_b", ln1g_d[0:1, :].partition_broadcast(128), [128, D])
            ln1b_b = load_small(sa, "ln1b_b", ln1b_d[0:1, :].partition_broadcast(128), [128, D])
            g1p1 = sa("g1p1", [128, 2, D])
            for b in range(2):
                DMA(g1p1[:, b, :], modb_d[:, (b * 4 + 0) * D:(b * 4 + 1) * D], ["modb_scr"], ["g1p1"])
            dg = sa("dg", [128, 20, 4, 128], BF16)
            for t_ in range(20):
                for k in range(4):
                    ts("dve", dg[:, t_, k, :], ident_f[:], convwT[:, t_, k:k + 1], None, ALU.mult, None,
                       ["ident_f", "convwT"], ["dg"])
            WB = 256
            wbuf = [sa("wbuf%d" % i, [128, 8, WB], BF16) for i in range(2)]
            wdt = sa("wdt", [128, 8, NH], BF16)
            DMA(wdt[:], winbf_d[:, :, 4096:4120].rearrange("k p n -> p k n"), ["winbf_scr"], ["wdt"])

            def two(name, shape, dt=F32):
                return [sa("%s_%d" % (name, i), shape, dt) for i in range(2)]

            u1T = two("u1T", [128, 8, SBT], BF16)
            xbcraw = two("xbcraw", [128, 20, SBT + 3], BF16)
            zs = two("zs", [128, NCH, DSSD], BF16)
            dts = two("dts", [128, NCH, NH])
            adts = two("adts", [128, NCH, NH])
            ys5T = two("ys5T", [128, 4, 128], BF16)
            xs = two("xs", [128, DSSD])
            xdt = two("xdt", [128, DSSD], BF16)
            xdte = two("xdte", [128, DSSD], BF16)
            Btok = two("Btok", [128, 512], BF16)
            BT = two("BT", [128, 4, 128], BF16)
            CT = two("CT", [128, 4, 128], BF16)
            sm = two("sm", [128, 8, NH])
            scm = two("scm", [128, 4, 128], BF16)
            ysum = two("ysum", [128, DSSD])
            yn = [sa("yn", [128, DSSD], BF16)] * 2
            ycat = [sa("ycat", [128, 12, 128], BF16)] * 2
            Rg2 = sa("Rg2", [128, 12, 128])
            decT2 = sa("decT2", [128, 12, 128], BF16)
            ytmp = two("ytmp", [128, 384])
            prev = sa("prev", [128, DSSD])
            prev_bf = sa("prev_bf", [128, DSSD], BF16)
            ssq = sa("ssq", [128, 8])
            xch = sa("xch", [128, D])
            stats = sa("stats", [128, 2, 6]); mv = sa("mv", [128, 2]); rstd = sa("rstd", [128, 1])
            DMA(xs[0][0:1, 0:1280], convb_d[0:1, 0:1280], [], ["xs_0"])
            cp("dve", convb_bf[0:1, 0:1280], xs[0][0:1, 0:1280], ["xs_0"], ["convb_bf"])
            DMA(xs[0][0:1, 0:1280], convb_d[0:1, 1280:2560], ["xs_0"], ["xs_0"])
            cp("dve", convb_bf[0:1, 1280:2560], xs[0][0:1, 0:1280], ["xs_0"], ["convb_bf"])

            SB_PER_SEQ = SEQ // SBT
            NSBG = 2 * SB_PER_SEQ
            NCHG = NSBG * NCH

            def nm(base, i):
                return "%s_%d" % (base, i)

            fillq = []
            fstate = {"old": 0}

            def fill(n):
                for _ in range(n):
                    if fillq:
                        fillq.pop(0)()
                        fstate["old"] = max(0, fstate["old"] - 1)

            def S0_pieces(sbg):
                b, sbi = sbg // SB_PER_SEQ, sbg % SB_PER_SEQ
                q = sbg % 2
                t0 = sbi * SBT
                FB = 3

                def pre():
                    if sbi == 0:
                        memset("pool", xbcraw[q][:, :, 0:3], 0.0, [nm("xbcraw", q)])
                    else:
                        cp("act", xbcraw[q][:, :, 0:3], xbcraw[1 - q][:, :, SBT:SBT + 3], [nm("xbcraw", 1 - q)], [nm("xbcraw", q)])
                    xTin4 = xch[:].rearrange("p (k t) -> p k t", k=4)
                    for kh in range(2):
                        xn = "xch"
                        DMA(xTin4, xT_d[b, kh * 512:(kh + 1) * 512, t0:t0 + SBT].rearrange("(k p) t -> p k t", p=128), [], [xn])
                        for k4 in range(4):
                            k = kh * 4 + k4
                            ts("dve", u1T[q][:, k, :], xTin4[:, k4, :], mod1T[:, 8 + k, b:b + 1], mod1T[:, k, b:b + 1],
                               ALU.mult, ALU.add, [xn, "mod1T"], [nm("u1T", q)])

                def blk(bi_):
                    slot = bi_ % 2
                    wn = "wbuf%d" % slot
                    col0 = bi_ * WB
                    DMA(wbuf[slot][:], winbf_d[:, :, col0:col0 + WB].rearrange("k p n -> p k n"), ["winbf_scr"], [wn])
                    if col0 < 1536:
                        for c in range(NCH):
                            for k in range(8):
                                mm(PS[FB][:, 0:WB], u1T[q][:, k, c * 128:(c + 1) * 128], wbuf[slot][:, k, :], k == 0, k == 7,
                                   [nm("u1T", q), wn], [pk(FB)])
                            cp("act", zs[q][:, c, col0:col0 + WB], PS[FB][:, 0:WB], [pk(FB)], [nm("zs", q)])
                    else:
                        for ct in range(WB // 128):
                            tile_ = (col0 - 1536) // 128 + ct
                            for k in range(8):
                                mm(PS[FB][:, 0:SBT], wbuf[slot][:, k, ct * 128:(ct + 1) * 128], u1T[q][:, k, :], k == 0, k == 7,
                                   [nm("u1T", q), wn], [pk(FB)])
                            cp("act", xbcraw[q][:, tile_, 3:SBT + 3], PS[FB][:, 0:SBT], [pk(FB)], [nm("xbcraw", q)])

                def dtp():
                    for c in range(NCH):
                        for k in range(8):
                            mm(PS[FB][:, 0:NH], u1T[q][:, k, c * 128:(c + 1) * 128], wdt[:, k, :], k == 0, k == 7, [nm("u1T", q), "wdt"], [pk(FB)])
                        tt("dve", dts[q][:, c, :], PS[FB][:, 0:NH], dtb_b[:], ALU.add, [pk(FB), "dtb_b"], [nm("dts", q)])
                    act(dts[q][:], dts[q][:], AF.Exp, [nm("dts", q)], [nm("dts", q)])
                    act(dts[q][:], dts[q][:], AF.Ln, [nm("dts", q)], [nm("dts", q)], bias=1.0)
                    tt("dve", adts[q][:], dts[q][:], a_b[:].unsqueeze(1).to_broadcast([128, NCH, NH]), ALU.mult, [nm("dts", q), "a_b"], [nm("adts", q)])

                return [pre] + [(lambda i=i: blk(i)) for i in range(6, 16)] + [dtp] + [(lambda i=i: blk(i)) for i in range(0, 6)]

            def S1a(ch):
                sbg, c = ch // NCH, ch % NCH
                q, p = sbg % 2, ch % 2
                w0 = c * 128
                xr, xrn = xbcraw[q], nm("xbcraw", q)
                for grp in range(4):
                    bank = 4 + (grp % 2)
                    mm(PS[bank][:, :], ones_bf[0:1, :], convb_bf[0:1, grp * 512:(grp + 1) * 512], True, False, ["ones_bf", "convb_bf"], [pk(bank)])
                    for t4 in range(4):
                        tile_ = grp * 4 + t4
                        o = PS[bank][:, t4 * 128:(t4 + 1) * 128]
                        for k in range(4):
                            mm(o, xr[:, tile_, w0 + k:w0 + k + 128], dg[:, tile_, k, :], False, (t4 == 3 and k == 3), [xrn, "dg"], [pk(bank)])
                    if grp < 3:
                        act(xs[p][:, grp * 512:(grp + 1) * 512], PS[bank][:, :], AF.Silu, [pk(bank)], [nm("xs", p)])
                    else:
                        act(Btok[p][:], PS[bank][:, :], AF.Silu, [pk(bank)], [nm("Btok", p)])
                for (dst, dname, tb0, bank) in ((BT[p], nm("BT", p), 12, 4), (CT[p], nm("CT", p), 16, 5)):
                    for gq in range(4):
                        tile_ = tb0 + gq
                        o = PS[bank][:, gq * 128:(gq + 1) * 128]
                        for k in range(4):
                            mm(o, dg[:, tile_, k, :], xr[:, tile_, w0 + k:w0 + k + 128], k == 0, k == 3, [xrn, "dg"], [pk(bank)])
                        act(dst[:, gq, :], o, AF.Silu, [pk(bank), "convbT"], [dname], bias=convbT[:, tile_:tile_ + 1])

            def S1b(ch):
                sbg, c = ch // NCH, ch % NCH
                q, p = sbg % 2, ch % 2
                smn = nm("sm", p)
                sm_ = sm[p]
                mm(PS[5][:, 0:NH], triU[:], adts[q][:, c, :], True, True, ["triU", nm("adts", q)], [pk(5)])
                mm(PS[5][:, 32:32 + NH], ones_f[:], adts[q][:, c, :], True, True, ["ones_f", nm("adts", q)], [pk(5)])
                cp("dve", sm_[:, 0, :], PS[5][:, 0:NH], [pk(5)], [smn])
                cp("dve", sm_[:, 1, :], PS[5][:, 32:32 + NH], [pk(5)], [smn])
                tt("dve", sm_[:, 6, :], sm_[:, 1, :], sm_[:, 0, :], ALU.subtract, [smn], [smn])
                act(sm_[:, 2, :], sm_[:, 0, :], AF.Exp, [smn], [smn])
                act(sm_[:, 3, :], sm_[:, 6, :], AF.Exp, [smn], [smn])
                act(sm_[:, 4, :], sm_[:, 1, :], AF.Exp, [smn], [smn])
                tt("dve", sm_[:, 5, :], dts[q][:, c, :], sm_[:, 3, :], ALU.mult, [nm("dts", q), smn], [smn])
                xs3 = xs[p][:].rearrange("p (h d) -> p h d", d=64)
                tt("dve", xdt[p][:].rearrange("p (h d) -> p h d", d=64), xs3, dts[q][:, c, :].unsqueeze(2).to_broadcast([128, NH, 64]),
                   ALU.mult, [nm("xs", p), nm("dts", q)], [nm("xdt", p)])
                tt("dve", xdte[p][:].rearrange("p (h d) -> p h d", d=64), xs3, sm_[:, 5, :].unsqueeze(2).to_broadcast([128, NH, 64]),
                   ALU.mult, [nm("xs", p), smn], [nm("xdte", p)])
                for g in range(4):
                    mm(PS[5][:, g * 128:(g + 1) * 128], BT[p][:, g, :], CT[p][:, g, :], True, True, [nm("BT", p), nm("CT", p)], [pk(5)])
                tt("dve", scm[p][:], PS[5][:, :].rearrange("p (g l) -> p g l", l=128), triU[:].unsqueeze(1).to_broadcast([128, 4, 128]),
                   ALU.mult, [pk(5), "triU"], [nm("scm", p)])

            def S2(ch):
                sbg, c = ch // NCH, ch % NCH
                q, p = sbg % 2, ch % 2
                smn = nm("sm", p)
                sm_ = sm[p]
                if ch % (NCH * SB_PER_SEQ) == 0:
                    memset("pool", prev[:], 0.0, ["prev%d" % g_ for g_ in range(4)]); memset("pool", prev_bf[:], 0.0, ["prev_bf%d" % g_ for g_ in range(4)])
                for pr2 in range(2):
                    h12 = slice(pr2 * 12, pr2 * 12 + 12)
                    tt("dve", Rg2[:], adts[q][:, c, h12].unsqueeze(2).to_broadcast([128, 12, 128]), triU[:].unsqueeze(1).to_broadcast([128, 12, 128]),
                       ALU.mult, [nm("adts", q), "triU"], ["Rg2"])
                    Rf = Rg2[:].rearrange("p j l -> p (j l)")
                    dT = decT2[:].rearrange("p j l -> p (j l)")
                    for i3 in range(3):
                        mm(PS[i3][:, :], strictL[:], Rf[:, i3 * 512:(i3 + 1) * 512], True, True, ["strictL", "Rg2"], [pk(i3)])
                    for i3 in range(3):
                        act(dT[:, i3 * 512:(i3 + 1) * 512], PS[i3][:, :], AF.Exp, [pk(i3)], ["decT2"])
                    tt("dve", decT2[:].rearrange("p (g j) l -> p g j l", g=2), decT2[:].rearrange("p (g j) l -> p g j l", g=2),
                       scm[p][:, 2 * pr2:2 * pr2 + 2, :].unsqueeze(2).to_broadcast([128, 2, 6, 128]), ALU.mult,
                       ["decT2", nm("scm", p)], ["decT2"])
                    fill(1)
                    for gi in range(2):
                        g = 2 * pr2 + gi
                        cs_ = slice(g * 384, g * 384 + 384)
                        yb, ob = 4 + gi, 6 + gi
                        for j in range(6):
                            h = g * 6 + j
                            mm(PS[yb][:, j * 64:(j + 1) * 64], decT2[:, gi * 6 + j, :], xdt[p][:, h * 64:(h + 1) * 64], True, True,
                               ["decT2", nm("xdt", p)], [pk(yb)])
                        mm(PS[ob][:, 0:384], CT[p][:, g, :], prev_bf[:, cs_], True, True, [nm("CT", p), "prev_bf%d" % g], [pk(ob)])
                    for gi in range(2):
                        g = 2 * pr2 + gi
                        hs = slice(g * 6, g * 6 + 6)
                        cs_ = slice(g * 384, g * 384 + 384)
                        yb, ob = 4 + gi, 6 + gi
                        tt("dve", ytmp[gi][:].rearrange("p (j d) -> p j d", d=64), PS[ob][:, 0:384].rearrange("p (j d) -> p j d", d=64),
                           sm_[:, 2, hs].unsqueeze(2).to_broadcast([128, 6, 64]), ALU.mult, [pk(ob), smn], [nm("ytmp", gi)])
                        tt("dve", ysum[p][:, cs_], PS[yb][:, 0:384], ytmp[gi][:], ALU.add, [pk(yb), nm("ytmp", gi)], [nm("ysum", p) + "g%d" % g])
                    fill(1)
                    for gi in range(2):
                        g = 2 * pr2 + gi
                        cs_ = slice(g * 384, g * 384 + 384)
                        ob = 6 + gi
                        mm(PS[ob][:, 0:384], Btok[p][:, g * 128:(g + 1) * 128], xdte[p][:, cs_], True, True, [nm("Btok", p), nm("xdte", p)], [pk(ob)])
                    for gi in range(2):
                        g = 2 * pr2 + gi
                        hs = slice(g * 6, g * 6 + 6)
                        cs_ = slice(g * 384, g * 384 + 384)
                        ob = 6 + gi
                        tt("dve", prev[:, cs_].rearrange("p (j d) -> p j d", d=64), prev[:, cs_].rearrange("p (j d) -> p j d", d=64),
                           sm_[:, 4, hs].unsqueeze(2).to_broadcast([128, 6, 64]), ALU.mult, ["prev%d" % g, smn], ["prev%d" % g])
                        tt("dve", prev[:, cs_], prev[:, cs_], PS[ob][:, 0:384], ALU.add, ["prev%d" % g, pk(ob)], ["prev%d" % g])
                        cp("act", prev_bf[:, cs_], prev[:, cs_], ["prev%d" % g], ["prev_bf%d" % g])

            def S3a(ch):
                sbg, c = ch // NCH, ch % NCH
                q, p = sbg % 2, ch % 2
                xsn, ysn = nm("xs", p), nm("ysum", p)
                act(zs[q][:, c, :], zs[q][:, c, :], AF.Silu, [nm("zs", q)], [nm("zs", q)])
                tt("dve", xs[p][:].rearrange("p (h d) -> p h d", d=64), xs[p][:].rearrange("p (h d) -> p h d", d=64), dfull[:].unsqueeze(2).to_broadcast([128, NH, 64]), ALU.mult, [xsn, "dfull"], [xsn])
                tt("dve", ysum[p][:], ysum[p][:], xs[p][:], ALU.add, [ysn, xsn] + [ysn + "g%d" % g_ for g_ in range(4)], [ysn] + [ysn + "g%d" % g_ for g_ in range(4)])
                tt("dve", ysum[p][:], ysum[p][:], zs[q][:, c, :], ALU.mult, [ysn, nm("zs", q)], [ysn])
                memset("dve", ssq[:, 0:4], 0.0, ["ssq"])
                for g in range(4):
                    cs_ = slice(g * 384, g * 384 + 384)
                    r_ = g % 2
                    A(lambda e, o=ytmp[r_][:], i_=ysum[p][:, cs_], acc=ssq[:, g:g + 1]: e.activation(out=o, in_=i_, func=AF.Square, accum_out=acc),
                      [ysn, "ssq"], [nm("ytmp", r_), "ssq"])
                ts("dve", ssq[:, 4:8], ssq[:, 0:4], 1.0 / 384.0, EPS, ALU.mult, ALU.add, ["ssq"], ["ssq"])
                act(ssq[:, 4:8], ssq[:, 4:8], AF.Ln, ["ssq"], ["ssq"])
                act(ssq[:, 4:8], ssq[:, 4:8], AF.Exp, ["ssq"], ["ssq"], scale=-0.5)
                tt("dve", yn[p][:].rearrange("p (g d) -> p g d", d=384), ysum[p][:].rearrange("p (g d) -> p g d", d=384),
                   ssq[:, 4:8].unsqueeze(2).to_broadcast([128, 4, 384]), ALU.mult, [ysn, "ssq"], ["yn"])

            def S3b(ch):
                sbg, c = ch // NCH, ch % NCH
                q, p = sbg % 2, ch % 2
                b, sbi = sbg // SB_PER_SEQ, sbg % SB_PER_SEQ
                t0 = sbi * SBT
                w0 = c * 128
                xsn, ysn = nm("xs", p), nm("ysum", p)
                fill(2)
                for t3 in range(3):
                    bank = 4 + (t3 % 2)
                    for t4 in range(4):
                        tile_ = t3 * 4 + t4
                        T(lambda e, o=PSB[bank][:, t4 * 128:(t4 + 1) * 128], i_=yn[p][:, tile_ * 128:(tile_ + 1) * 128]:
                          e.transpose(out=o, in_=i_, identity=ident_bf[:]), ["yn", "ident_bf"], [pk(bank)])
                    cp("act" if t3 % 2 else "dve", ycat[p][:, t3 * 4:t3 * 4 + 4, :],
                       PSB[bank][:, 0:512].rearrange("p (t l) -> p t l", l=128), [pk(bank)], ["ycat"])
                rr_ = xs[p]
                x1t = ysum[p]
                DMA(xch[:], x_d[b, t0 + w0:t0 + w0 + 128, :], [], ["xch"])
                DMA(ys5T[p][:], ys5_d[b, :, :, t0 + w0:t0 + w0 + 128].rearrange("q p t -> p q t"), ["ys5_scr"], [nm("ys5T", p)])
                for half in range(2):
                    bank = 6 + half
                    for kt in range(16):
                        if kt < 12:
                            lhs, ln_ = ycat[p][:, kt, :], "ycat"
                        else:
                            lhs, ln_ = ys5T[p][:, kt - 12, :], nm("ys5T", p)
                        mm(PS[bank][:, :], lhs, wout_bf[:, kt, half * 512:(half + 1) * 512], kt == 0, kt == 15,
                           [ln_, "wout_bf"], [pk(bank)])
                    hsl = slice(half * 512, (half + 1) * 512)
                    tt("dve", rr_[:, hsl], PS[bank][:, :], g1p1[:, b, hsl], ALU.mult, [pk(bank), "g1p1"], [xsn])
                    stt(rr_[:, hsl], xch[:, hsl], ALPHA, rr_[:, hsl], ALU.mult, ALU.add, ["xch", xsn], [xsn])
                    V(lambda e, o=stats[:, half, :], i_=rr_[:, hsl]: e.bn_stats(out=o, in_=i_), [xsn], ["stats"])
                V(lambda e: e.bn_aggr(out=mv[:], in_=stats[:].rearrange("p a s -> p (a s)")), ["stats"], ["mv"])
                ts("dve", rstd[:], mv[:, 1:2], EPS, None, ALU.add, None, ["mv"], ["rstd"])
                act(rstd[:], rstd[:], AF.Ln, ["rstd"], ["rstd"])
                act(rstd[:], rstd[:], AF.Exp, ["rstd"], ["rstd"], scale=-0.5)
                fill(2)
                ts("dve", rr_[:, 0:D], rr_[:, 0:D], mv[:, 0:1], rstd[:, 0:1], ALU.subtract, ALU.mult, [xsn, "mv", "rstd"], [xsn])
                tt("dve", rr_[:, 0:D], rr_[:, 0:D], ln1g_b[:], ALU.mult, [xsn, "ln1g_b"], [xsn])
                tt("dve", x1t[:, 0:D], rr_[:, 0:D], ln1b_b[:], ALU.add, [xsn, "ln1b_b"], [ysn])
                DMA(x1_d[b, t0 + w0:t0 + w0 + 128, :], x1t[:, 0:D], [ysn], ["x1_scr"], chan=("x1w", p))

            for pc_ in S0_pieces(0):
                pc_()
            S1a(0)
            S1b(0)
            for ch in range(NCHG):
                nxt = ch + 1
                if ch % NCH == 0 and (ch // NCH + 1) < NSBG:
                    fstate["old"] = len(fillq)
                    fillq.extend(S0_pieces(ch // NCH + 1))
                    fill(fstate["old"] + 1)
                S2(ch)
                if nxt < NCHG:
                    if nxt % NCH == 0:
                        fill(max(0, len(fillq) - 6))
                    S1a(nxt)
                S3a(ch)
                if nxt < NCHG:
                    S1b(nxt)
                S3b(ch)
            fill(len(fillq))
            S.barrier(mk_bar, ["x1_scr"])

        if "B" in phases:
          with contextlib.ExitStack() as stB:
            def sbb(name, shape, dt=F32):
                return stB.enter_context(nc.sbuf_tensor("t%d_%s" % (next(_uid), name), list(shape), dt))

            w1_bf = sbb("w1_bf", [128, 8, DFF], BF16)
            w2_bf = sbb("w2_bf", [128, 32, D], BF16)
            wld = [sbb("wld%d" % i, [128, 512]) for i in range(3)]
            b1T = sbb("b1T", [128, 32])
            DMA(b1T[:], b1T_d[:, :], [], ["b1T"])
            b2bf = sbb("b2bf", [1, D], BF16)
            ln2g_b = sbb("ln2g_b", [128, D]); ln2b_b = sbb("ln2b_b", [128, D])
            DMA(ln2g_b[:], ln2g_d[0:1, :].partition_broadcast(128), [], ["ln2g_b"])
            DMA(ln2b_b[:], ln2b_d[0:1, :].partition_broadcast(128), [], ["ln2b_b"])
            modU = sbb("modU", [128, 2, D])
            modG = sbb("modG", [128, D])
            n_ = 0
            for k in range(8):
                for hf in range(8):
                    wn = "wld%d" % (n_ % 3)
                    DMA(wld[n_ % 3][:], w1_d[k * 128:(k + 1) * 128, hf * 512:(hf + 1) * 512], [], [wn])
                    cp("dve" if (n_ % 2) else "act", w1_bf[:, k, hf * 512:(hf + 1) * 512], wld[n_ % 3][:], [wn], ["w1_bf"])
                    n_ += 1
            for f2 in range(64):
                wn = "wld%d" % (n_ % 3)
                fq, fh = f2 // 2, f2 % 2
                DMA(wld[n_ % 3][:], w2_d[fq * 128:(fq + 1) * 128, fh * 512:(fh + 1) * 512], [], [wn])
                cp("dve" if (n_ % 2) else "act", w2_bf[:, fq, fh * 512:(fh + 1) * 512], wld[n_ % 3][:], [wn], ["w2_bf"])
                n_ += 1
            x1b_ = [sbb("x1b%d" % i, [128, 2, D]) for i in range(2)]
            u2 = sbb("u2", [128, D], BF16)
            u2T_ = [sbb("u2T%d" % i, [128, 8, 256], BF16) for i in range(2)]
            hT = sbb("hT", [128, 32, 256], BF16)
            htmp = [sbb("htmp%d" % i, [128, 256]) for i in range(2)]
            r2 = sbb("r2", [128, D]); ot = r2
            stats2 = sbb("stats2", [128, 2, 6]); mv2 = sbb("mv2", [128, 2]); rstd2 = sbb("rstd2", [128, 1])
            NBLK = 2 * NSB * 2
            DMA(r2[0:1, :], b2_d[0:1, :], [], ["r2"])
            cp("dve", b2bf[:], r2[0:1, :], ["r2"], ["b2bf"])

            def blk_info(i):
                b = i // (NSB * 2)
                t0 = (i % (NSB * 2)) * 256
                return b, t0, i % 2

            def B0_load(i):
                b, t0, pa = blk_info(i)
                DMA(x1b_[pa][:], x1_d[b, t0:t0 + 256, :].rearrange("(c p) d -> p c d", p=128), ["x1_scr"], ["x1b%d" % pa])

            def B0_u2(i, c):
                b, t0, pa = blk_info(i)
                if t0 == 0 and c == 0:
                    DMA(modU[:], modb_d[:, (b * 4 + 1) * D:(b * 4 + 3) * D].rearrange("p (q d) -> p q d", d=D), ["modb_scr"], ["modU"])
                tt("dve", u2[:], x1b_[pa][:, c, :], modU[:, 1, :], ALU.mult, ["x1b%d" % pa, "modU"], ["u2"])
                tt("dve", u2[:], u2[:], modU[:, 0, :], ALU.add, ["u2", "modU"], ["u2"])
                for kq in range(2):
                    bank = kq
                    for k4 in range(4):
                        k = kq * 4 + k4
                        T(lambda e, o=PSB[bank][:, k4 * 128:(k4 + 1) * 128], i_=u2[:, k * 128:(k + 1) * 128]:
                          e.transpose(out=o, in_=i_, identity=ident_bf[:]), ["u2", "ident_bf"], [pk(bank)])
                    cp("act", u2T_[pa][:, kq * 4:kq * 4 + 4, c * 128:(c + 1) * 128],
                       PSB[bank][:, 0:512].rearrange("p (t l) -> p t l", l=128), [pk(bank)], ["u2T%d" % pa])

            def B1(i):
                b, t0, pa = blk_info(i)
                for f in range(32):
                    bank = 2 + (f % 4)
                    for k in range(8):
                        mm(PS[bank][:, 0:256], w1_bf[:, k, f * 128:(f + 1) * 128], u2T_[pa][:, k, :], k == 0, k == 7, ["w1_bf", "u2T%d" % pa], [pk(bank)])
                    hn = "htmp%d" % (f % 2)
                    act(htmp[f % 2][:], PS[bank][:, 0:256], AF.Identity, [pk(bank), "b1T"], [hn], bias=b1T[:, f:f + 1])
                    stt(hT[:, f, :], htmp[f % 2][:], 0.0, htmp[f % 2][:], ALU.max, ALU.mult, [hn], ["hT"])
                    if i + 1 < NBLK and f in (6, 20):
                        B0_u2(i + 1, 0 if f == 6 else 1)

            def B2(i):
                b, t0, pa = blk_info(i)
                x1b = x1b_[pa]
                xn = "x1b%d" % pa
                if t0 == 0:
                    DMA(modG[:], modb_d[:, (b * 4 + 3) * D:(b * 4 + 4) * D], ["modb_scr"], ["modG"])
                for c in range(2):
                    for half in range(2):
                        bank = 6 + half
                        hsl = slice(half * 512, (half + 1) * 512)
                        for f in range(32):
                            mm(PS[bank][:, :], hT[:, f, c * 128:(c + 1) * 128], w2_bf[:, f, hsl], f == 0, False, ["hT", "w2_bf"], [pk(bank)])
                        mm(PS[bank][:, :], ones_bf[0:1, :], b2bf[0:1, hsl], False, True, ["ones_bf", "b2bf"], [pk(bank)])
                        tt("dve", r2[:, hsl], PS[bank][:, :], modG[:, hsl], ALU.mult, [pk(bank), "modG"], ["r2"])
                        stt(r2[:, hsl], x1b[:, c, hsl], ALPHA, r2[:, hsl], ALU.mult, ALU.add, [xn, "r2"], ["r2"])
                        V(lambda e, o=stats2[:, half, :], i_=r2[:, hsl]: e.bn_stats(out=o, in_=i_), ["r2"], ["stats2"])
                    V(lambda e: e.bn_aggr(out=mv2[:], in_=stats2[:].rearrange("p a s -> p (a s)")), ["stats2"], ["mv2"])
                    ts("dve", rstd2[:], mv2[:, 1:2], EPS, None, ALU.add, None, ["mv2"], ["rstd2"])
                    act(rstd2[:], rstd2[:], AF.Ln, ["rstd2"], ["rstd2"])
                    act(rstd2[:], rstd2[:], AF.Exp, ["rstd2"], ["rstd2"], scale=-0.5)
                    ts("dve", r2[:], r2[:], mv2[:, 0:1], rstd2[:, 0:1], ALU.subtract, ALU.mult, ["r2", "mv2", "rstd2"], ["r2"])
                    tt("dve", r2[:], r2[:], ln2g_b[:], ALU.mult, ["r2", "ln2g_b"], ["r2"])
                    tt("dve", ot[:], r2[:], ln2b_b[:], ALU.add, ["r2", "ln2b_b"], ["r2"])
                    DMA(out_d[b, t0 + c * 128:t0 + (c + 1) * 128, :], ot[:], ["r2"], [], chan=("outw",))

            B0_load(0)
            B0_u2(0, 0)
            B0_u2(0, 1)
            for i in range(NBLK):
                if i + 1 < NBLK:
                    B0_load(i + 1)
                B1(i)
                B2(i)
            out_chans.append(("outw",))

        if "x1" in dbg_d:
            pass
        S.finalize()
        S.emit(final_wait_chans=out_chans)
    return nc


def _l1(a):
    a = np.asarray(a)
    rest = a.shape[2:]
    a = a.reshape((16, 2, 64) + rest)
    perm = (1, 2, 0) + tuple(range(3, 3 + len(rest)))
    return np.ascontiguousarray(a.transpose(perm).reshape((128, 16) + rest))


def make_in_maps(inputs, NSB=8, n_cores=8):
    f = lambda a: np.ascontiguousarray(np.asarray(a, dtype=np.float32))
    SEQ = NSB * 512
    x = f(inputs["x"])[:, :SEQ]
    c = f(inputs["c"])
    shared = {
        "w_ada": f(inputs["w_ada"][0]),
        "b_ada": f(inputs["b_ada"][0]).reshape(1, -1),
        "b_adaT": f(f(inputs["b_ada"][0])[:2048].reshape(16, 128).T),
        "w_in": f(inputs["w_in"][0]),
        "convwT": f(f(inputs["conv_w"][0]).reshape(4, 20, 128).transpose(2, 1, 0)),
        "conv_b": f(inputs["conv_b"][0]).reshape(1, -1),
        "convbT": f(f(inputs["conv_b"][0]).reshape(20, 128).T),
        "dt_bias": f(inputs["dt_bias"][0]).reshape(1, -1),
        "a_log": f(inputs["a_log"][0]).reshape(1, -1),
        "dfull": f(np.repeat(f(inputs["d_ssd"][0]), 64)).reshape(1, -1),
        "d_ssd": f(inputs["d_ssd"][0]).reshape(1, -1),
        "normwT": f(f(inputs["norm_w"][0]).reshape(12, 128).T),
        "s5ar": _l1(f(inputs["s5_a_re"][0])),
        "s5ai": _l1(f(inputs["s5_a_im"][0])),
        "s5ldt": _l1(np.repeat(f(inputs["s5_log_dt"][0])[:, None], 64, axis=1)),
        "s5br": _l1(f(inputs["s5_b_re"][0])),
        "s5bi": _l1(f(inputs["s5_b_im"][0])),
        "s5cr": _l1(f(inputs["s5_c_re"][0]).transpose(0, 2, 1)),
        "s5ci": _l1(f(inputs["s5_c_im"][0]).transpose(0, 2, 1)),
        "s5dv": f(np.tile(f(inputs["s5_d"][0]).T, (8, 1))),
        "w_glu": f(inputs["w_glu"][0]),
        "b_gluT": f(f(inputs["b_glu"][0]).reshape(4, 128).T),
        "w_out": f(inputs["w_out"][0]),
        "ln1_g": f(inputs["ln1_g"][0]).reshape(1, -1),
        "ln1_b": f(inputs["ln1_b"][0]).reshape(1, -1),
        "w1": f(inputs["w1"][0]),
        "b1T": f(f(inputs["b1"][0]).reshape(32, 128).T),
        "w2": f(inputs["w2"][0]),
        "b2": f(inputs["b2"][0]).reshape(1, -1),
        "ln2_g": f(inputs["ln2_g"][0]).reshape(1, -1),
        "ln2_b": f(inputs["ln2_b"][0]).reshape(1, -1),
    }
    maps = []
    for i in range(n_cores):
        xb = x[2 * i:2 * i + 2]
        m = dict(shared)
        m["x"] = np.ascontiguousarray(xb)
        m["xT"] = np.ascontiguousarray(xb.transpose(0, 2, 1))
        m["cT"] = np.ascontiguousarray(c[2 * i:2 * i + 2].reshape(2, 8, 128).transpose(2, 1, 0))
        maps.append(m)
    return maps


_NC_CACHE = {}


def kernel(**inputs):
    NSB = 8
    if NSB not in _NC_CACHE:
        _NC_CACHE[NSB] = build_program(NSB)
    nc = _NC_CACHE[NSB]
    maps = make_in_maps(inputs, NSB)
    res = run_bass_kernel_spmd(nc, maps, core_ids=list(range(8)))
    out = np.concatenate([np.asarray(r["out"]) for r in res.results], axis=0)
    return out.astype(np.float32)
```

```python
import contextlib
import math
import numpy as np
import concourse.bass as bass
import concourse.mybir as mybir
from concourse.bass_utils import run_bass_kernel_spmd

F32 = mybir.dt.float32
BF16 = mybir.dt.bfloat16
I32 = mybir.dt.int32
AF = mybir.ActivationFunctionType
ALU = mybir.AluOpType

D = 1024
DSSD = 1536
NH = 24
DIN = 4632
DFF = 4096
ALPHA = 2.0 ** 0.25
EPS = 1e-5
TWO_PI = 2.0 * math.pi
SAME_ENG_SYNC = True


class _Op:
    __slots__ = ("eng", "fn", "reads", "writes", "dma", "chan", "waits", "signal", "val", "idx")

    def __init__(self, eng, fn, reads, writes, dma, chan):
        self.eng = eng
        self.fn = fn
        self.reads = reads
        self.writes = writes
        self.dma = dma
        self.chan = chan
        self.waits = []
        self.signal = False
        self.val = None


class Sched:
    ENGS = ("pe", "act", "dve", "pool", "sp")

    def __init__(self, nc):
        self.nc = nc
        self.ops = []

    def add(self, eng, fn, reads=(), writes=(), dma=False, chan=None):
        op = _Op(eng, fn, tuple(reads), tuple(writes), dma, chan)
        op.idx = len(self.ops)
        self.ops.append(op)
        return op

    def dma(self, eng, fn, reads=(), writes=(), chan=None):
        if chan is None:
            chan = ("auto", tuple(writes) if writes else tuple(reads))
        return self.add(eng, fn, reads, writes, dma=True, chan=chan)

    def barrier(self, mk, extra_reads=()):
        n = getattr(self, "_bar_n", 0)
        self._bar_n = n + 1
        names = []
        for E in ("pe", "act", "dve", "pool"):
            nm = "bar%d_%s" % (n, E)
            names.append(nm)
            self.add(E, mk(E, 0), (["ones_f"] if E == "act" else ["ones_bf"] if E == "pe" else []), [nm] + (["ps7"] if E == "pe" else []))
        for E in ("pe", "act", "dve", "pool"):
            self.add(E, mk(E, 1), names + list(extra_reads) + (["ones_f"] if E == "act" else ["ones_bf"] if E == "pe" else []),
                     ["bar%d_b_%s" % (n, E)] + (["ps7"] if E == "pe" else []))
        self.dma("sp", mk("sp", 1), names + list(extra_reads), ["bar_d"], chan=("bar", n))

    def finalize(self):
        last_w = {}
        readers = {}
        deps_all = []
        for op in self.ops:
            deps = set()
            for r in op.reads:
                w = last_w.get(r)
                if w is not None:
                    deps.add(w)
            for r in op.writes:
                w = last_w.get(r)
                if w is not None:
                    deps.add(w)
                for rd in readers.get(r, ()):
                    deps.add(rd)
            deps.discard(op.idx)
            for r in op.reads:
                readers.setdefault(r, []).append(op.idx)
            for r in op.writes:
                last_w[r] = op.idx
                readers[r] = []
            need = []
            for d in sorted(deps):
                Dp = self.ops[d]
                if (not Dp.dma) and Dp.eng == op.eng:
                    if op.eng == "pe" or not SAME_ENG_SYNC:
                        continue
                need.append(d)
            deps_all.append(need)
            for d in need:
                self.ops[d].signal = True
        cnt = {e: 0 for e in self.ENGS}
        chan_cnt = {}
        for op in self.ops:
            if op.dma:
                c = chan_cnt.get(op.chan, 0) + 1
                chan_cnt[op.chan] = c
                op.val = 16 * c
            elif op.signal:
                cnt[op.eng] += 1
                op.val = cnt[op.eng]
        self.chans = list(chan_cnt.keys())
        waited = {e: {} for e in self.ENGS}
        for op, need in zip(self.ops, deps_all):
            for d in need:
                Dp = self.ops[d]
                key = ("c", Dp.chan) if Dp.dma else ("e", Dp.eng)
                if waited[op.eng].get(key, 0) >= Dp.val:
                    continue
                waited[op.eng][key] = Dp.val
                op.waits.append((key, Dp.val))
        return cnt, chan_cnt

    def emit(self, final_wait_chans=()):
        nc = self.nc
        with contextlib.ExitStack() as st:
            sems = {}
            for e in self.ENGS:
                sems[("e", e)] = st.enter_context(nc.semaphore("s_" + e))
            for i, c in enumerate(self.chans):
                sems[("c", c)] = st.enter_context(nc.semaphore("d%d" % i))
            block = st.enter_context(nc.Block())
            by_eng = {e: [op for op in self.ops if op.eng == e] for e in self.ENGS}
            chan_final = {}
            for op in self.ops:
                if op.dma:
                    chan_final[op.chan] = op.val

            def run(engobj, ename):
                for op in by_eng[ename]:
                    for key, val in op.waits:
                        engobj.wait_ge(sems[key], val)
                    ins = op.fn(engobj)
                    if op.dma:
                        ins.then_inc(sems[("c", op.chan)], 16)
                    elif op.signal:
                        ins.then_inc(sems[("e", ename)], 1)
                if ename == "sp":
                    for c in final_wait_chans:
                        engobj.wait_ge(sems[("c", c)], chan_final[c])

            @block.tensor
            def _(e):
                run(e, "pe")

            @block.scalar
            def _(e):
                run(e, "act")

            @block.vector
            def _(e):
                run(e, "dve")

            @block.gpsimd
            def _(e):
                run(e, "pool")

            @block.sync
            def _(e):
                run(e, "sp")


def build_program(NSB=8, dbg=(), phases=("A1", "A2", "B")):
    SEQ = NSB * 512
    nc = bass.Bass("TRN2", target_bir_lowering=False)
    S = Sched(nc)
    _uid = iter(range(100000))

    def din(name, shape, dt=F32):
        return nc.dram_tensor(name, list(shape), dt, kind="ExternalInput").ap()

    def dscr(name, shape, dt=F32):
        return nc.dram_tensor(name, list(shape), dt, kind="Internal").ap()

    xT_d = din("xT", [2, D, SEQ])
    x_d = din("x", [2, SEQ, D])
    cT_d = din("cT", [128, 8, 2])
    w_ada_d = din("w_ada", [D, 6 * D])
    b_ada_d = din("b_ada", [1, 6 * D])
    b_adaT_d = din("b_adaT", [128, 16])
    w_in_d = din("w_in", [D, DIN])
    convwT_d = din("convwT", [128, 20, 4])
    convb_d = din("conv_b", [1, 2560])
    convbT_d = din("convbT", [128, 20])
    dtb_d = din("dt_bias", [1, NH])
    alog_d = din("a_log", [1, NH])
    dfull_d = din("dfull", [1, DSSD])
    dssd_d = din("d_ssd", [1, NH])
    normwT_d = din("normwT", [128, 12])
    s5ar_d = din("s5ar", [128, 16])
    s5ai_d = din("s5ai", [128, 16])
    s5ldt_d = din("s5ldt", [128, 16])
    s5br_d = din("s5br", [128, 16, 16])
    s5bi_d = din("s5bi", [128, 16, 16])
    s5cr_d = din("s5cr", [128, 16, 16])
    s5ci_d = din("s5ci", [128, 16, 16])
    s5dv_d = din("s5dv", [128, 32])
    w_glu_d = din("w_glu", [512, 512])
    b_gluT_d = din("b_gluT", [128, 4])
    w_out_d = din("w_out", [2048, D])
    ln1g_d = din("ln1_g", [1, D])
    ln1b_d = din("ln1_b", [1, D])
    w1_d = din("w1", [D, DFF])
    b1T_d = din("b1T", [128, 32])
    w2_d = din("w2", [DFF, D])
    b2_d = din("b2", [1, D])
    ln2g_d = din("ln2_g", [1, D])
    ln2b_d = din("ln2_b", [1, D])

    out_d = nc.dram_tensor("out", [2, SEQ, D], F32, kind="ExternalOutput").ap()
    x1_d = dscr("x1_scr", [2, SEQ, D])
    winbf_d = dscr("winbf_scr", [8, 128, DIN], BF16)
    woutbf_d = dscr("woutbf_scr", [16, 128, D], BF16)
    modb_d = dscr("modb_scr", [128, 8 * D])
    ys5_d = dscr("ys5_scr", [2, 4, 128, SEQ], BF16)
    dbg_d = {}
    for name, shape in dbg:
        dbg_d[name] = nc.dram_tensor("dbg_" + name, list(shape), F32, kind="ExternalOutput").ap()

    out_chans = []

    with contextlib.ExitStack() as st:
        def sb(name, shape, dt=F32):
            return st.enter_context(nc.sbuf_tensor("t%d_%s" % (next(_uid), name), list(shape), dt))

        def V(fn, reads, writes):
            return S.add("dve", fn, reads, writes)

        def G(fn, reads, writes):
            return S.add("pool", fn, reads, writes)

        def A(fn, reads, writes):
            return S.add("act", fn, reads, writes)

        def T(fn, reads, writes):
            return S.add("pe", fn, reads, writes)

        def DMA(out, in_, reads, writes, chan=None, q="sp"):
            return S.dma(q, lambda e, o=out, i=in_: e.dma_start(out=o, in_=i), reads, writes, chan=chan)

        def mm(out, lhsT, rhs, start, stop, reads, writes):
            return T(lambda e, o=out, l=lhsT, r=rhs, a=start, b=stop: e.matmul(o, lhsT=l, rhs=r, start=a, stop=b),
                     reads, writes)

        def tt(eng, out, in0, in1, op, reads, writes):
            return S.add(eng, lambda e, o=out, a=in0, b=in1, p=op: e.tensor_tensor(out=o, in0=a, in1=b, op=p),
                         reads, writes)

        def ts(eng, out, in0, s1, s2, op0, op1, reads, writes):
            if op1 is None:
                return S.add(eng, lambda e, o=out, a=in0, x=s1, p=op0: e.tensor_scalar(out=o, in0=a, scalar1=x, scalar2=None, op0=p),
                             reads, writes)
            return S.add(eng, lambda e, o=out, a=in0, x=s1, y=s2, p=op0, q=op1: e.tensor_scalar(out=o, in0=a, scalar1=x, scalar2=y, op0=p, op1=q),
                         reads, writes)

        def stt(out, in0, scalar, in1, op0, op1, reads, writes):
            return V(lambda e, o=out, a=in0, s=scalar, b=in1, p=op0, q=op1: e.scalar_tensor_tensor(out=o, in0=a, scalar=s, in1=b, op0=p, op1=q),
                     reads, writes)

        def act(out, in_, func, reads, writes, bias=None, scale=None):
            kw = {}
            if bias is not None:
                kw["bias"] = bias
            if scale is not None:
                kw["scale"] = scale
            return A(lambda e, o=out, i=in_, f=func, k=kw: e.activation(out=o, in_=i, func=f, **k), reads, writes)

        def cp(eng, out, in_, reads, writes):
            if eng == "act":
                return A(lambda e, o=out, i=in_: e.copy(out=o, in_=i), reads, writes)
            return S.add(eng, lambda e, o=out, i=in_: e.tensor_copy(out=o, in_=i), reads, writes)

        def memset(eng, ap, val, writes):
            return S.add(eng, lambda e, a=ap, v=val: e.memset(a, v), (), writes)

        PS = [st.enter_context(nc.psum_tensor("ps%d" % i, [128, 512], F32)) for i in range(8)]
        PSB = [p[:].bitcast(BF16) for p in PS]

        def pk(i):
            return "ps%d" % i

        ident_bf = sb("ident_bf", [128, 128], BF16)
        ident_f = sb("ident_f", [128, 128])
        triU = sb("triU", [128, 128])
        strictL = sb("strictL", [128, 128])
        ones_f = sb("ones_f", [128, 128])
        ones_bf = sb("ones_bf", [1, 128], BF16)
        bar_t = sb("bar_t", [128, 16])
        bar_d = sb("bar_d", [1, 4])

        def mk_bar(E, tag):
            col = {"act": 0, "dve": 2, "pool": 4}.get(E, 6) + tag
            if E == "pe":
                return lambda e: e.matmul(PS[7][0:1, 0:1], lhsT=ones_bf[0:1, 0:1], rhs=ones_bf[0:1, 0:1], start=True, stop=True)
            if E == "sp":
                return lambda e: e.dma_start(out=bar_d[0:1, 0:4], in_=b_ada_d[0:1, 0:4])
            if E == "act":
                return lambda e, c=col: e.copy(out=bar_t[:, c:c + 1], in_=ones_f[:, 0:1])
            return lambda e, c=col: e.memset(bar_t[:, c:c + 1], 0.0)

        memset("pool", ident_f[:], 0.0, ["ident_f"])
        G(lambda e: e.affine_select(out=ident_f[:], in_=ident_f[:], pattern=[[-1, 128]], compare_op=ALU.not_equal,
                                    fill=1.0, base=0, channel_multiplier=1), ["ident_f"], ["ident_f"])
        cp("dve", ident_bf[:], ident_f[:], ["ident_f"], ["ident_bf"])
        memset("pool", ones_f[:], 1.0, ["ones_f"])
        memset("pool", ones_bf[:], 1.0, ["ones_bf"])
        G(lambda e: e.affine_select(out=triU[:], in_=ones_f[:], pattern=[[1, 128]], compare_op=ALU.is_ge,
                                    fill=0.0, base=0, channel_multiplier=-1), ["ones_f"], ["triU"])
        G(lambda e: e.affine_select(out=strictL[:], in_=ones_f[:], pattern=[[-1, 128]], compare_op=ALU.is_gt,
                                    fill=0.0, base=0, channel_multiplier=1), ["ones_f"], ["strictL"])

        def load_small(alloc, name, src, shape, dt=F32):
            t = alloc(name, shape, dt)
            DMA(t[:], src, [], [name])
            return t

        mod1T = sb("mod1T", [128, 16, 2])
        b_gluT = load_small(sb, "b_gluT", b_gluT_d[:, :], [128, 4])

        with contextlib.ExitStack() as stA1:
            def sa1(name, shape, dt=F32):
                return stA1.enter_context(nc.sbuf_tensor("t%d_%s" % (next(_uid), name), list(shape), dt))

            Tg = sa1("Tg", [128, 32, 128], BF16)
            Wre = sa1("Wre", [128, 32, 128], BF16)
            Wim = sa1("Wim", [128, 32, 128], BF16)
            Gre = sa1("Gre", [128, 16, 128], BF16)
            Gim = sa1("Gim", [128, 16, 128], BF16)
            ctab = sa1("ctab", [128, 16, 64])
            stab = sa1("stab", [128, 16, 64])
            r8 = sa1("r8", [128, 16])
            rho8tab = sa1("rho8tab", [128, 16, 64])
            wglu_bf = sa1("wglu_bf", [128, 4, 512], BF16)
            ws5 = sa1("ws5", [128, 8, 512], BF16)

            with contextlib.ExitStack() as st5:
                def s5t(name, shape, dt=F32):
                    return st5.enter_context(nc.sbuf_tensor("t%d_%s" % (next(_uid), name), list(shape), dt))

                b_adaT = load_small(s5t, "b_adaT", b_adaT_d[:, :], [128, 16])
                condT = s5t("condT", [128, 8, 2])
                DMA(condT[:], cT_d[:, :, :], [], ["condT"])
                act(condT[:], condT[:], AF.Silu, ["condT"], ["condT"])
                condrep = s5t("condrep", [128, 8, 2, 128])
                for b in range(2):
                    cp("dve", condrep[:, :, b, :], condT[:, :, b:b + 1].to_broadcast([128, 8, 128]), ["condT"], ["condrep"])
                modb = s5t("modb", [128, 2, 4, D])
                badab = s5t("badab", [128, 4 * D])
                DMA(badab[:], b_ada_d[0:1, 2 * D:6 * D].partition_broadcast(128), [], ["badab"])
                wa = [s5t("wa%d" % i, [128, 8, 512]) for i in range(2)]
                for blk in range(12):
                    wt = wa[blk % 2]
                    wn = "wa%d" % (blk % 2)
                    DMA(wt[:], w_ada_d[:, blk * 512:(blk + 1) * 512].rearrange("(k p) n -> p k n", p=128), [], [wn])
                    if blk < 4:
                        for jt in range(4):
                            j = blk * 4 + jt
                            bank = 4 + (j % 2)
                            for k in range(8):
                                mm(PS[bank][:, 0:2], wt[:, k, jt * 128:(jt + 1) * 128], condT[:, k, :], k == 0, k == 7,
                                   [wn, "condT"], [pk(bank)])
                            ts("dve", mod1T[:, j, :], PS[bank][:, 0:2], b_adaT[:, j:j + 1], None, ALU.add, None,
                               [pk(bank), "b_adaT"], ["mod1T"])
                    else:
                        q = (blk - 4) // 2
                        half = (blk - 4) % 2
                        for b in range(2):
                            bank = 6 + b
                            for k in range(8):
                                mm(PS[bank][:, :], condrep[:, k, b, :], wt[:, k, :], k == 0, k == 7, [wn, "condrep"], [pk(bank)])
                            tt("dve", modb[:, b, q, half * 512:(half + 1) * 512], PS[bank][:, :],
                               badab[:, q * D + half * 512: q * D + (half + 1) * 512], ALU.add, [pk(bank), "badab"], ["modb"])
                ts("dve", mod1T[:, 8:16, :], mod1T[:, 8:16, :], 1.0, None, ALU.add, None, ["mod1T"], ["mod1T"])
                for b in range(2):
                    for q in (0, 2, 3):
                        ts("dve", modb[:, b, q, :], modb[:, b, q, :], 1.0, None, ALU.add, None, ["modb"], ["modb"])
                DMA(modb_d[:, :], modb[:].rearrange("p b q d -> p (b q d)"), ["modb"], ["modb_scr"], chan=("modw",))
                S.barrier(mk_bar, ["modb_scr"])

            with contextlib.ExitStack() as st5:
                def s5t(name, shape, dt=F32):
                    return st5.enter_context(nc.sbuf_tensor("t%d_%s" % (next(_uid), name), list(shape), dt))

                normwT = load_small(s5t, "normwT", normwT_d[:, :], [128, 12])
                NSL = 4
                wl = [s5t("wb%d" % i, [128, 8, 512]) for i in range(NSL)]
                wc = [s5t("wc%d" % i, [128, 4096], BF16) for i in range(NSL)]
                n_ = 0
                for k in range(8):
                    for (c0, c1) in ((0, 2316), (2316, DIN)):
                        ln_, cn_ = "wb%d" % (n_ % NSL), "wc%d" % (n_ % NSL)
                        w_ = c1 - c0
                        wlf = wl[n_ % NSL][:].rearrange("p a b -> p (a b)")
                        DMA(wlf[:, 0:w_], w_in_d[k * 128:(k + 1) * 128, c0:c1], [], [ln_])
                        cp("dve", wc[n_ % NSL][:, 0:1158], wlf[:, 0:1158], [ln_], [cn_])
                        cp("dve", wc[n_ % NSL][:, 1158:w_], wlf[:, 1158:w_], [ln_], [cn_])
                        DMA(winbf_d[k, :, c0:c1], wc[n_ % NSL][:, 0:w_], [cn_], ["winbf_scr"], chan=("wscr", n_ % NSL))
                        if c1 == DIN:
                            cp("act", ws5[:, k, :], wlf[:, 4120 - 2316:4632 - 2316], [ln_], ["ws5"])
                        n_ += 1
                for kt in range(16):
                    ln_, cn_ = "wb%d" % (n_ % NSL), "wc%d" % (n_ % NSL)
                    wlf = wl[n_ % NSL][:].rearrange("p a b -> p (a b)")
                    DMA(wlf[:, 0:D], w_out_d[kt * 128:(kt + 1) * 128, :], [], [ln_])
                    if kt < 12:
                        ts("dve", wc[n_ % NSL][:, 0:D], wlf[:, 0:D], normwT[:, kt:kt + 1], None, ALU.mult, None,
                           [ln_, "normwT"], [cn_])
                    else:
                        cp("dve" if kt % 2 else "act", wc[n_ % NSL][:, 0:D], wlf[:, 0:D], [ln_], [cn_])
                    DMA(woutbf_d[kt], wc[n_ % NSL][:, 0:D], [cn_], ["woutbf_scr"], chan=("wscr", n_ % NSL))
                    n_ += 1
                for kt in range(4):
                    ln_ = "wb%d" % (n_ % NSL)
                    wlf = wl[n_ % NSL][:].rearrange("p a b -> p (a b)")
                    DMA(wlf[:, 0:512], w_glu_d[kt * 128:(kt + 1) * 128, :], [], [ln_])
                    cp("dve", wglu_bf[:, kt, :], wlf[:, 0:512], [ln_], ["wglu_bf"])
                    n_ += 1

                S.barrier(mk_bar, ["winbf_scr", "woutbf_scr"])

            with contextlib.ExitStack() as st5:
                def s5t(name, shape, dt=F32):
                    return st5.enter_context(nc.sbuf_tensor("t%d_%s" % (next(_uid), name), list(shape), dt))

                ar = s5t("s5_ar", [128, 16]); ai = s5t("s5_ai", [128, 16]); dtg = s5t("s5_dt", [128, 16])
                DMA(ar[:], s5ar_d[:, :], [], ["s5_ar"]); DMA(ai[:], s5ai_d[:, :], [], ["s5_ai"]); DMA(dtg[:], s5ldt_d[:, :], [], ["s5_dt"])
                br = s5t("s5_br", [128, 16, 16]); bi = s5t("s5_bi", [128, 16, 16])
                cr = s5t("s5_cr", [128, 16, 16]); ci = s5t("s5_ci", [128, 16, 16])
                dv = s5t("s5_dv", [128, 32])
                DMA(br[:], s5br_d[:, :, :], [], ["s5_br"]); DMA(bi[:], s5bi_d[:, :, :], [], ["s5_bi"])
                DMA(cr[:], s5cr_d[:, :, :], [], ["s5_cr"]); DMA(ci[:], s5ci_d[:, :, :], [], ["s5_ci"])
                DMA(dv[:], s5dv_d[:, :], [], ["s5_dv"])
                act(dtg[:], dtg[:], AF.Exp, ["s5_dt"], ["s5_dt"])
                rho = s5t("s5_rho", [128, 16]); th = s5t("s5_th", [128, 16])
                tmp = [s5t("s5_tmp%d" % i, [128, 16]) for i in range(6)]
                tmpi = s5t("s5_tmpi", [128, 16], I32)
                tt("dve", rho[:], ar[:], dtg[:], ALU.mult, ["s5_ar", "s5_dt"], ["s5_rho"])
                act(rho[:], rho[:], AF.Exp, ["s5_rho"], ["s5_rho"])
                tt("dve", th[:], ai[:], dtg[:], ALU.mult, ["s5_ai", "s5_dt"], ["s5_th"])

                def sin_of(out, src, shift, rs, ws):
                    ts("dve", tmp[0][:], src, shift, 1.0 / TWO_PI, ALU.add, ALU.mult, rs, ["s5_tmp0"])
                    cp("dve", tmpi[:], tmp[0][:], ["s5_tmp0"], ["s5_tmpi"])
                    cp("dve", tmp[1][:], tmpi[:], ["s5_tmpi"], ["s5_tmp1"])
                    tt("dve", tmp[0][:], tmp[0][:], tmp[1][:], ALU.subtract, ["s5_tmp0", "s5_tmp1"], ["s5_tmp0"])
                    ts("dve", tmp[0][:], tmp[0][:], TWO_PI, 3.14159, ALU.mult, ALU.min, ["s5_tmp0"], ["s5_tmp0"])
                    ts("dve", tmp[0][:], tmp[0][:], -3.14159, None, ALU.max, None, ["s5_tmp0"], ["s5_tmp0"])
                    act(out, tmp[0][:], AF.Sin, ["s5_tmp0"], ws)

                cs = s5t("s5_cs", [128, 16]); sn = s5t("s5_sn", [128, 16])
                sin_of(sn[:], th[:], 0.0, ["s5_th"], ["s5_sn"])
                sin_of(cs[:], th[:], math.pi / 2.0, ["s5_th"], ["s5_cs"])
                abr = s5t("s5_abr", [128, 16]); abi = s5t("s5_abi", [128, 16])
                tt("dve", abr[:], rho[:], cs[:], ALU.mult, ["s5_rho", "s5_cs"], ["s5_abr"])
                tt("dve", abi[:], rho[:], sn[:], ALU.mult, ["s5_rho", "s5_sn"], ["s5_abi"])
                den = tmp[2]; nre = tmp[3]; cfr = s5t("s5_cfr", [128, 16]); cfi = s5t("s5_cfi", [128, 16])
                tt("dve", den[:], ar[:], ar[:], ALU.mult, ["s5_ar"], ["s5_tmp2"])
                tt("dve", tmp[4][:], ai[:], ai[:], ALU.mult, ["s5_ai"], ["s5_tmp4"])
                tt("dve", den[:], den[:], tmp[4][:], ALU.add, ["s5_tmp2", "s5_tmp4"], ["s5_tmp2"])
                V(lambda e: e.reciprocal(out=den[:], in_=den[:]), ["s5_tmp2"], ["s5_tmp2"])
                ts("dve", nre[:], abr[:], -1.0, None, ALU.add, None, ["s5_abr"], ["s5_tmp3"])
                tt("dve", cfr[:], nre[:], ar[:], ALU.mult, ["s5_tmp3", "s5_ar"], ["s5_cfr"])
                tt("dve", tmp[4][:], abi[:], ai[:], ALU.mult, ["s5_abi", "s5_ai"], ["s5_tmp4"])
                tt("dve", cfr[:], cfr[:], tmp[4][:], ALU.add, ["s5_cfr", "s5_tmp4"], ["s5_cfr"])
                tt("dve", cfr[:], cfr[:], den[:], ALU.mult, ["s5_cfr", "s5_tmp2"], ["s5_cfr"])
                tt("dve", cfi[:], abi[:], ar[:], ALU.mult, ["s5_abi", "s5_ar"], ["s5_cfi"])
                tt("dve", tmp[4][:], nre[:], ai[:], ALU.mult, ["s5_tmp3", "s5_ai"], ["s5_tmp4"])
                tt("dve", cfi[:], cfi[:], tmp[4][:], ALU.subtract, ["s5_cfi", "s5_tmp4"], ["s5_cfi"])
                tt("dve", cfi[:], cfi[:], den[:], ALU.mult, ["s5_cfi", "s5_tmp2"], ["s5_cfi"])
                Bbr = s5t("s5_Bbr", [128, 16, 16]); Bbi = s5t("s5_Bbi", [128, 16, 16])
                t3a = s5t("s5_t3a", [128, 16, 16])

                def bc16(a):
                    return a[:].unsqueeze(2).to_broadcast([128, 16, 16])

                tt("dve", Bbr[:], br[:], bc16(cfr), ALU.mult, ["s5_br", "s5_cfr"], ["s5_Bbr"])
                tt("dve", t3a[:], bi[:], bc16(cfi), ALU.mult, ["s5_bi", "s5_cfi"], ["s5_t3a"])
                tt("dve", Bbr[:], Bbr[:], t3a[:], ALU.subtract, ["s5_Bbr", "s5_t3a"], ["s5_Bbr"])
                tt("dve", Bbi[:], bi[:], bc16(cfr), ALU.mult, ["s5_bi", "s5_cfr"], ["s5_Bbi"])
                tt("dve", t3a[:], br[:], bc16(cfi), ALU.mult, ["s5_br", "s5_cfi"], ["s5_t3a"])
                tt("dve", Bbi[:], Bbi[:], t3a[:], ALU.add, ["s5_Bbi", "s5_t3a"], ["s5_Bbi"])
                Pr = s5t("s5_Pr", [128, 16, 9]); Pi = s5t("s5_Pi", [128, 16, 9])
                memset("dve", Pr[:, :, 0:1], 1.0, ["s5_Pr"]); memset("dve", Pi[:, :, 0:1], 0.0, ["s5_Pi"])

                def cmul_small(outr, outi, ar_, ai_, br_, bi_, rs, ws):
                    tt("dve", tmp[4][:], ar_, br_, ALU.mult, rs, ["s5_tmp4"])
                    tt("dve", tmp[5][:], ai_, bi_, ALU.mult, rs, ["s5_tmp5"])
                    tt("dve", tmp[0][:], ar_, bi_, ALU.mult, rs, ["s5_tmp0"])
                    tt("dve", tmp[1][:], ai_, br_, ALU.mult, rs, ["s5_tmp1"])
                    tt("dve", outr, tmp[4][:], tmp[5][:], ALU.subtract, ["s5_tmp4", "s5_tmp5"], ws)
                    tt("dve", outi, tmp[0][:], tmp[1][:], ALU.add, ["s5_tmp0", "s5_tmp1"], ws)

                for k in range(1, 9):
                    cmul_small(Pr[:, :, k], Pi[:, :, k], Pr[:, :, k - 1], Pi[:, :, k - 1], abr[:], abi[:],
                               ["s5_Pr", "s5_Pi", "s5_abr", "s5_abi"], ["s5_Pr", "s5_Pi"])
                Qre = s5t("s5_Qre", [128, 16, 16, 16]); Qie = s5t("s5_Qie", [128, 16, 16, 16])
                t4a = s5t("s5_t4a", [128, 16, 9, 16])
                memset("pool", Qre[:, :, 0:7, :], 0.0, ["s5_Qre"]); memset("pool", Qie[:, :, 0:7, :], 0.0, ["s5_Qie"])

                def cb(a):
                    return a[:].unsqueeze(2).to_broadcast([128, 16, 9, 16])

                def pb(a):
                    return a[:].unsqueeze(3).to_broadcast([128, 16, 9, 16])

                tt("dve", Qre[:, :, 7:16, :], cb(cr), pb(Pr), ALU.mult, ["s5_cr", "s5_Pr"], ["s5_Qre"])
                tt("dve", t4a[:], cb(ci), pb(Pi), ALU.mult, ["s5_ci", "s5_Pi"], ["s5_t4a"])
                tt("dve", Qre[:, :, 7:16, :], Qre[:, :, 7:16, :], t4a[:], ALU.subtract, ["s5_Qre", "s5_t4a"], ["s5_Qre"])
                tt("dve", Qie[:, :, 7:16, :], cb(cr), pb(Pi), ALU.mult, ["s5_cr", "s5_Pi"], ["s5_Qie"])
                tt("dve", t4a[:], cb(ci), pb(Pr), ALU.mult, ["s5_ci", "s5_Pr"], ["s5_t4a"])
                tt("dve", Qie[:, :, 7:16, :], Qie[:, :, 7:16, :], t4a[:], ALU.add, ["s5_Qie", "s5_t4a"], ["s5_Qie"])
                ts("dve", Qie[:, :, 7:16, :], Qie[:, :, 7:16, :], -1.0, None, ALU.mult, None, ["s5_Qie"], ["s5_Qie"])
                cp("dve", Gre[:].rearrange("p a (j h) -> p a j h", h=16), Qre[:, :, 8:16, :], ["s5_Qre"], ["Gre"])
                cp("dve", Gim[:].rearrange("p a (j h) -> p a j h", h=16), Qie[:, :, 8:16, :], ["s5_Qie"], ["Gim"])
                WTr = s5t("s5_WTr", [128, 16, 8, 16]); WTi = s5t("s5_WTi", [128, 16, 8, 16])
                for i in range(8):
                    k = 7 - i
                    pr_b = Pr[:, :, k:k + 1].to_broadcast([128, 16, 16])
                    pi_b = Pi[:, :, k:k + 1].to_broadcast([128, 16, 16])
                    tt("dve", WTr[:, :, i, :], Bbr[:], pr_b, ALU.mult, ["s5_Bbr", "s5_Pr"], ["s5_WTr"])
                    tt("dve", t3a[:], Bbi[:], pi_b, ALU.mult, ["s5_Bbi", "s5_Pi"], ["s5_t3a"])
                    tt("dve", WTr[:, :, i, :], WTr[:, :, i, :], t3a[:], ALU.subtract, ["s5_WTr", "s5_t3a"], ["s5_WTr"])
                    tt("dve", WTi[:, :, i, :], Bbi[:], pr_b, ALU.mult, ["s5_Bbi", "s5_Pr"], ["s5_WTi"])
                    tt("dve", t3a[:], Bbr[:], pi_b, ALU.mult, ["s5_Bbr", "s5_Pi"], ["s5_t3a"])
                    tt("dve", WTi[:, :, i, :], WTi[:, :, i, :], t3a[:], ALU.add, ["s5_WTi", "s5_t3a"], ["s5_WTi"])
                memset("pool", Wre[:], 0.0, ["Wre"]); memset("pool", Wim[:], 0.0, ["Wim"])
                for pr_ in range(16):
                    for (src, dst, sname, dname, bank) in ((WTr, Wre, "s5_WTr", "Wre", 0), (WTi, Wim, "s5_WTi", "Wim", 1)):
                        T(lambda e, o=PS[bank][:, 0:128], i_=src[:, pr_, :, :].rearrange("p i h -> p (i h)"): e.transpose(out=o, in_=i_, identity=ident_f[:]),
                          [sname, "ident_f"], [pk(bank)])
                        cp("dve", dst[:, 2 * pr_, 0:64], PS[bank][:, 0:64], [pk(bank)], [dname])
                        cp("act", dst[:, 2 * pr_ + 1, 64:128], PS[bank][:, 64:128], [pk(bank)], [dname])
                BbZr = s5t("s5_BbZr", [128, 16, 15, 16]); BbZi = s5t("s5_BbZi", [128, 16, 15, 16])
                memset("pool", BbZr[:], 0.0, ["s5_BbZr"]); memset("pool", BbZi[:], 0.0, ["s5_BbZi"])
                cp("dve", BbZr[:, :, 7, :], Bbr[:], ["s5_Bbr"], ["s5_BbZr"])
                cp("dve", BbZi[:, :, 7, :], Bbi[:], ["s5_Bbi"], ["s5_BbZi"])
                BbZr_b = s5t("s5_BbZr_b", [128, 16, 15, 16], BF16); BbZi_b = s5t("s5_BbZi_b", [128, 16, 15, 16], BF16)
                Qre_b = s5t("s5_Qre_b", [128, 16, 16, 16], BF16); Qie_b = s5t("s5_Qie_b", [128, 16, 16, 16], BF16)
                cp("dve", BbZr_b[:], BbZr[:], ["s5_BbZr"], ["s5_BbZr_b"]); cp("act", BbZi_b[:], BbZi[:], ["s5_BbZi"], ["s5_BbZi_b"])
                cp("dve", Qre_b[:], Qre[:], ["s5_Qre"], ["s5_Qre_b"]); cp("act", Qie_b[:], Qie[:], ["s5_Qie"], ["s5_Qie_b"])
                for g in range(32):
                    pr_, g2 = g // 2, g % 2
                    bank = 2 + (g % 2)
                    lo, hi = 64 * g2, 64 * g2 + 64
                    n = 0
                    for i in range(8):
                        for (bz, qq, bzn, qn) in ((BbZr_b, Qre_b, "s5_BbZr_b", "s5_Qre_b"), (BbZi_b, Qie_b, "s5_BbZi_b", "s5_Qie_b")):
                            lhs = bz[lo:hi, pr_, 7 - i:7 - i + 8, :].rearrange("p a h -> p (a h)")
                            rhs = qq[lo:hi, pr_, 7 - i:7 - i + 8, :].rearrange("p a h -> p (a h)")
                            mm(PS[bank][:, 0:128], lhs, rhs, n == 0, n == 15, [bzn, qn], [pk(bank)])
                            n += 1
                    stt(Tg[:, g, :], ident_f[:], dv[:, g:g + 1], PS[bank][:, 0:128], ALU.mult, ALU.add,
                        ["ident_f", "s5_dv", pk(bank)], ["Tg"])
                e8r = s5t("s5_e8r", [128, 16]); e8i = s5t("s5_e8i", [128, 16])
                cp("dve", e8r[:], cs[:], ["s5_cs"], ["s5_e8r"]); cp("dve", e8i[:], sn[:], ["s5_sn"], ["s5_e8i"])
                cp("dve", r8[:], rho[:], ["s5_rho"], ["r8"])
                e2r = s5t("s5_e2r", [128, 16]); e2i = s5t("s5_e2i", [128, 16])
                for _ in range(3):
                    cmul_small(e2r[:], e2i[:], e8r[:], e8i[:], e8r[:], e8i[:], ["s5_e8r", "s5_e8i"], ["s5_e2r", "s5_e2i"])
                    cp("dve", e8r[:], e2r[:], ["s5_e2r"], ["s5_e8r"]); cp("dve", e8i[:], e2i[:], ["s5_e2i"], ["s5_e8i"])
                    tt("dve", r8[:], r8[:], r8[:], ALU.mult, ["r8"], ["r8"])
                cp("dve", rho8tab[:], r8[:].unsqueeze(2).to_broadcast([128, 16, 64]), ["r8"], ["rho8tab"])
                memset("dve", rho8tab[:, :, 0:1], 0.0, ["rho8tab"])
                cp("dve", ctab[:, :, 0], e8r[:], ["s5_e8r"], ["ctab"]); cp("dve", stab[:, :, 0], e8i[:], ["s5_e8i"], ["stab"])
                tb = [s5t("s5_tb%d" % i, [128, 16, 32]) for i in range(4)]
                m = 1
                while m < 64:
                    er = e8r[:].unsqueeze(2).to_broadcast([128, 16, m]); ei = e8i[:].unsqueeze(2).to_broadcast([128, 16, m])
                    tt("dve", tb[0][:, :, 0:m], ctab[:, :, 0:m], er, ALU.mult, ["ctab", "s5_e8r"], ["s5_tb0"])
                    tt("dve", tb[1][:, :, 0:m], stab[:, :, 0:m], ei, ALU.mult, ["stab", "s5_e8i"], ["s5_tb1"])
                    tt("dve", tb[2][:, :, 0:m], ctab[:, :, 0:m], ei, ALU.mult, ["ctab", "s5_e8i"], ["s5_tb2"])
                    tt("dve", tb[3][:, :, 0:m], stab[:, :, 0:m], er, ALU.mult, ["stab", "s5_e8r"], ["s5_tb3"])
                    tt("dve", ctab[:, :, m:2 * m], tb[0][:, :, 0:m], tb[1][:, :, 0:m], ALU.subtract, ["s5_tb0", "s5_tb1"], ["ctab"])
                    tt("dve", stab[:, :, m:2 * m], tb[2][:, :, 0:m], tb[3][:, :, 0:m], ALU.add, ["s5_tb2", "s5_tb3"], ["stab"])
                    cmul_small(e2r[:], e2i[:], e8r[:], e8i[:], e8r[:], e8i[:], ["s5_e8r", "s5_e8i"], ["s5_e2r", "s5_e2i"])
                    cp("dve", e8r[:], e2r[:], ["s5_e2r"], ["s5_e8r"]); cp("dve", e8i[:], e2i[:], ["s5_e2i"], ["s5_e8i"])
                    m *= 2
                if "Tg" in dbg_d:
                    tgf = s5t("dbg_tgf", [128, 32 * 128])
                    cp("dve", tgf[:], Tg[:].rearrange("p g c -> p (g c)"), ["Tg"], ["dbg_tgf"])
                    DMA(dbg_d["Tg"][:, :], tgf[:], ["dbg_tgf"], [], chan=("dbg", "Tg")); out_chans.append(("dbg", "Tg"))
                if "ctab" in dbg_d:
                    DMA(dbg_d["ctab"][:, :], ctab[:].rearrange("p a b -> p (a b)"), ["ctab"], [], chan=("dbg", "ctab")); out_chans.append(("dbg", "ctab"))
                S.barrier(mk_bar, [])

            if "A1" in phases:
                xTin = [sa1("xTin%d" % i, [128, 512]) for i in range(2)]
                u1T_ = [sa1("u1T%d" % i, [128, 8, 512], BF16) for i in range(2)]
                UT8g_ = [sa1("UT8g%d" % i, [64, 32, 8, 16], BF16) for i in range(2)]
                Ug = sa1("Ug", [128, 32, 64], BF16)
                s5w = [sa1("s5w%d" % i, [128, 16, 64]) for i in range(6)]
                Sin_re = sa1("Sin_re", [128, 16]); Sin_im = sa1("Sin_im", [128, 16])
                fixr = sa1("fixr", [128, 16]); fixi = sa1("fixi", [128, 16])
                SfBr = sa1("SfBr", [128, 16, 65], BF16); SfBi = sa1("SfBi", [128, 16, 65], BF16)
                Ygt = sa1("Ygt", [64, 8, 512], BF16)
                y5a = sa1("y5a", [64, 1024])
                ygT = sa1("ygT", [128, 4, 512], BF16)
                gsig = sa1("gsig", [128, 512])
                yo5 = sa1("yo5", [128, 4, 512], BF16)
                def P_prep(si):
                    b, sbi = si // NSB, si % NSB
                    t0 = sbi * 512
                    pa = si % 2
                    for k in range(8):
                        xn = "xTin%d" % (k % 2)
                        DMA(xTin[k % 2][:], xT_d[b, k * 128:(k + 1) * 128, t0:t0 + 512], [], [xn])
                        ts("dve", u1T_[pa][:, k, :], xTin[k % 2][:], mod1T[:, 8 + k, b:b + 1], mod1T[:, k, b:b + 1],
                           ALU.mult, ALU.add, [xn, "mod1T"], ["u1T%d" % pa])

                def P_mm(si):
                    pa = si % 2
                    for i in range(8):
                        bank = 6 + (i % 2)
                        for k in range(8):
                            mm(PS[bank][0:64, :], u1T_[pa][:, k, i::8], ws5[:, k, :], k == 0, k == 7, ["u1T%d" % pa, "ws5"], [pk(bank)])
                        cp("act", UT8g_[pa][:, :, i, :], PS[bank][0:64, :].rearrange("p (g h) -> p g h", h=16),
                           [pk(bank)], ["UT8g%d" % pa])

                P_prep(0)
                P_mm(0)
                for si in range(2 * NSB):
                    if True:
                        b, sbi = si // NSB, si % NSB
                        t0 = sbi * 512
                        pa = si % 2
                        UT8g = UT8g_[pa]
                        if sbi == 0:
                            memset("pool", Sin_re[:], 0.0, ["Sin_re"]); memset("pool", Sin_im[:], 0.0, ["Sin_im"])
                        if si + 1 < 2 * NSB:
                            P_prep(si + 1)
                        for g in range(32):
                            bank = g // 16
                            T(lambda e, o=PSB[bank][:, (g % 16) * 64:(g % 16) * 64 + 64], i_=UT8g[:, g, :, :].rearrange("p i h -> p (i h)"):
                              e.transpose(out=o, in_=i_, identity=ident_bf[0:64, 0:64]), ["UT8g%d" % pa, "ident_bf"], [pk(bank)])
                        cp("dve", Ug[:, 0:16, :], PSB[0][:, :].rearrange("p (g b) -> p g b", b=64), [pk(0)], ["Ug"])
                        cp("act", Ug[:, 16:32, :], PSB[1][:, :].rearrange("p (g b) -> p g b", b=64), [pk(1)], ["Ug"])
                        for (Wx, wname, b0) in ((Wre, "Wre", 0), (Wim, "Wim", 2)):
                            for pr_ in range(16):
                                bank = b0 + pr_ // 8
                                o = PS[bank][:, (pr_ % 8) * 64:(pr_ % 8) * 64 + 64]
                                mm(o, Wx[:, 2 * pr_, :], Ug[:, 2 * pr_, :], True, False, [wname, "Ug"], [pk(bank)])
                                mm(o, Wx[:, 2 * pr_ + 1, :], Ug[:, 2 * pr_ + 1, :], False, True, [wname, "Ug"], [pk(bank)])
                        if si + 1 < 2 * NSB:
                            P_mm(si + 1)
                        for h_ in range(2):
                            Er = PS[0 + h_][:, :].rearrange("p (a b) -> p a b", b=64)
                            Ei = PS[2 + h_][:, :].rearrange("p (a b) -> p a b", b=64)
                            sl = slice(h_ * 8, h_ * 8 + 8)
                            tt("dve", s5w[0][:, sl, :], Er, ctab[:, sl, :], ALU.mult, [pk(h_), "ctab"], ["s5w0"])
                            tt("dve", s5w[1][:, sl, :], Ei, stab[:, sl, :], ALU.mult, [pk(2 + h_), "stab"], ["s5w1"])
                            tt("dve", s5w[2][:, sl, :], Ei, ctab[:, sl, :], ALU.mult, [pk(2 + h_), "ctab"], ["s5w2"])
                            tt("dve", s5w[3][:, sl, :], Er, stab[:, sl, :], ALU.mult, [pk(h_), "stab"], ["s5w3"])
                        tt("dve", s5w[0][:], s5w[0][:], s5w[1][:], ALU.add, ["s5w0", "s5w1"], ["s5w0"])
                        tt("dve", s5w[2][:], s5w[2][:], s5w[3][:], ALU.subtract, ["s5w2", "s5w3"], ["s5w2"])
                        cp("dve", SfBr[:, :, 0], Sin_re[:], ["Sin_re"], ["SfBr"])
                        cp("dve", SfBi[:, :, 0], Sin_im[:], ["Sin_im"], ["SfBi"])
                        tt("dve", fixr[:], r8[:], Sin_re[:], ALU.mult, ["r8", "Sin_re"], ["fixr"])
                        tt("dve", s5w[0][:, :, 0], s5w[0][:, :, 0], fixr[:], ALU.add, ["s5w0", "fixr"], ["s5w0"])
                        tt("dve", fixi[:], r8[:], Sin_im[:], ALU.mult, ["r8", "Sin_im"], ["fixi"])
                        tt("dve", s5w[2][:, :, 0], s5w[2][:, :, 0], fixi[:], ALU.add, ["s5w2", "fixi"], ["s5w2"])
                        rtf = rho8tab[:].rearrange("p a b -> p (a b)")
                        V(lambda e, o=s5w[4][:].rearrange("p a b -> p (a b)"), d0=rtf, d1=s5w[0][:].rearrange("p a b -> p (a b)"):
                          e.tensor_tensor_scan(out=o, data0=d0, data1=d1, initial=0.0, op0=ALU.mult, op1=ALU.add),
                          ["rho8tab", "s5w0"], ["s5w4"])
                        V(lambda e, o=s5w[5][:].rearrange("p a b -> p (a b)"), d0=rtf, d1=s5w[2][:].rearrange("p a b -> p (a b)"):
                          e.tensor_tensor_scan(out=o, data0=d0, data1=d1, initial=0.0, op0=ALU.mult, op1=ALU.add),
                          ["rho8tab", "s5w2"], ["s5w5"])
                        tt("dve", s5w[0][:], s5w[4][:], ctab[:], ALU.mult, ["s5w4", "ctab"], ["s5w0"])
                        tt("dve", s5w[1][:], s5w[5][:], stab[:], ALU.mult, ["s5w5", "stab"], ["s5w1"])
                        tt("dve", s5w[2][:], s5w[5][:], ctab[:], ALU.mult, ["s5w5", "ctab"], ["s5w2"])
                        tt("dve", s5w[3][:], s5w[4][:], stab[:], ALU.mult, ["s5w4", "stab"], ["s5w3"])
                        tt("dve", s5w[0][:], s5w[0][:], s5w[1][:], ALU.subtract, ["s5w0", "s5w1"], ["s5w0"])
                        tt("dve", s5w[2][:], s5w[2][:], s5w[3][:], ALU.add, ["s5w2", "s5w3"], ["s5w2"])
                        cp("dve", SfBr[:, :, 1:65], s5w[0][:], ["s5w0"], ["SfBr"])
                        cp("dve", SfBi[:, :, 1:65], s5w[2][:], ["s5w2"], ["SfBi"])
                        cp("dve", Sin_re[:], s5w[0][:, :, 63], ["s5w0"], ["Sin_re"])
                        cp("dve", Sin_im[:], s5w[2][:, :, 63], ["s5w2"], ["Sin_im"])
                        for q in range(4):
                            for gg in range(8):
                                g = q * 8 + gg
                                pr_, g2 = g // 2, g % 2
                                lo, hi = 64 * g2, 64 * g2 + 64
                                bank = gg // 4
                                o = PS[bank][0:64, (gg % 4) * 128:(gg % 4) * 128 + 128]
                                mm(o, Ug[:, g, :], Tg[:, g, :], True, False, ["Ug", "Tg"], [pk(bank)])
                                mm(o, SfBr[lo:hi, pr_, 0:64], Gre[lo:hi, pr_, :], False, False, ["SfBr", "Gre"], [pk(bank)])
                                mm(o, SfBi[lo:hi, pr_, 0:64], Gim[lo:hi, pr_, :], False, True, ["SfBi", "Gim"], [pk(bank)])
                            for hb in range(2):
                                yp = PS[hb][0:64, :]
                                o = Ygt[:, :, q * 128 + hb * 64: q * 128 + hb * 64 + 64].rearrange("p j (g h) -> p g j h", h=16)
                                act(o, yp.rearrange("p (g j h) -> p g j h", j=8, h=16), AF.Gelu_apprx_tanh, [pk(hb)], ["Ygt"])
                        for q in range(4):
                            for j in range(8):
                                bank = 2 + (j // 4) % 2
                                T(lambda e, o=PSB[bank][:, (j % 4) * 64:(j % 4) * 64 + 64], i_=Ygt[:, j, q * 128:(q + 1) * 128]:
                                  e.transpose(out=o, in_=i_, identity=ident_bf[0:64, 0:64]), ["Ygt", "ident_bf"], [pk(bank)])
                                if j % 4 == 3:
                                    jj = j // 4
                                    cp("act" if jj else "dve",
                                       ygT[:, q, :].rearrange("p (b j) -> p j b", j=8)[:, jj * 4:jj * 4 + 4, :],
                                       PSB[bank][:, 0:256].rearrange("p (j b) -> p j b", b=64), [pk(bank)], ["ygT"])
                        for m_ in range(4):
                            bank = 4 + (m_ % 2)
                            for q in range(4):
                                mm(PS[bank][:, :], wglu_bf[:, q, m_ * 128:(m_ + 1) * 128], ygT[:, q, :], q == 0, q == 3, ["wglu_bf", "ygT"], [pk(bank)])
                            act(gsig[:], PS[bank][:, :], AF.Sigmoid, [pk(bank), "b_gluT"], ["gsig"], bias=b_gluT[:, m_:m_ + 1])
                            tt("dve", yo5[:, m_, :], gsig[:], ygT[:, m_, :], ALU.mult, ["gsig", "ygT"], ["yo5"])
                        DMA(ys5_d[b, :, :, t0:t0 + 512].rearrange("q p t -> p q t"), yo5[:], ["yo5"], ["ys5_scr"], chan=("ys5w",), q="pool")
                        if "ys5" in dbg_d and si == 0:
                            dbt = sa1("dbg_ys5", [128, 4, 512])
                            cp("dve", dbt[:], yo5[:], ["yo5"], ["dbg_ys5"])
                            DMA(dbg_d["ys5"][:, :], dbt[:].rearrange("p q t -> p (q t)"), ["dbg_ys5"], [], chan=("dbg", "ys5")); out_chans.append(("dbg", "ys5"))
            S.barrier(mk_bar, ["ys5_scr"])

        SBT = 256
        NCH = SBT // 128
        if "A2" in phases:
          with contextlib.ExitStack() as stA:
            def sa(name, shape, dt=F32):
                return stA.enter_context(nc.sbuf_tensor("t%d_%s" % (next(_uid), name), list(shape), dt))

            wout_bf = sa("wout_bf", [128, 16, D], BF16)
            DMA(wout_bf[:], woutbf_d[:, :, :].rearrange("k p n -> p k n"), ["woutbf_scr"], ["wout_bf"])
            convwT = load_small(sa, "convwT", convwT_d[:, :, :], [128, 20, 4])
            convbT = load_small(sa, "convbT", convbT_d[:, :], [128, 20])
            convb_bf = sa("convb_bf", [1, 2560], BF16)
            dtb_b = load_small(sa, "dtb_b", dtb_d[0:1, :].partition_broadcast(128), [128, NH])
            a_b = load_small(sa, "a_b", alog_d[0:1, :].partition_broadcast(128), [128, NH])
            act(a_b[:], a_b[:], AF.Exp, ["a_b"], ["a_b"])
            ts("dve", a_b[:], a_b[:], -1.0, None, ALU.mult, None, ["a_b"], ["a_b"])
            dfull = load_small(sa, "dfull", dssd_d[0:1, :].partition_broadcast(128), [128, NH])
            ln1g_b = load_small(sa, "ln1g_b", ln1g_d[0:1, :].partition_broadcast(128), [128, D])
            ln1b_b = load_small(sa, "ln1b_b", ln1b_d[0:1, :].partition_broadcast(128), [128, D])
            g1p1 = sa("g1p1", [128, 2, D])
            for b in range(2):
                DMA(g1p1[:, b, :], modb_d[:, (b * 4 + 0) * D:(b * 4 + 1) * D], ["modb_scr"], ["g1p1"])
            dg = sa("dg", [128, 20, 4, 128], BF16)
            for t_ in range(20):
                for k in range(4):
                    ts("dve", dg[:, t_, k, :], ident_f[:], convwT[:, t_, k:k + 1], None, ALU.mult, None,
                       ["ident_f", "convwT"], ["dg"])
            WB = 256
            wbuf = [sa("wbuf%d" % i, [128, 8, WB], BF16) for i in range(2)]
            wdt = sa("wdt", [128, 8, NH], BF16)
            DMA(wdt[:], winbf_d[:, :, 4096:4120].rearrange("k p n -> p k n"), ["winbf_scr"], ["wdt"])

            def two(name, shape, dt=F32):
                return [sa("%s_%d" % (name, i), shape, dt) for i in range(2)]

            u1T = two("u1T", [128, 8, SBT], BF16)
            xbcraw = two("xbcraw", [128, 20, SBT + 3], BF16)
            zs = two("zs", [128, NCH, DSSD], BF16)
            dts = two("dts", [128, NCH, NH])
            adts = two("adts", [128, NCH, NH])
            ys5T = two("ys5T", [128, 4, 128], BF16)
            xs = two("xs", [128, DSSD])
            xdt = two("xdt", [128, DSSD], BF16)
            xdte = two("xdte", [128, DSSD], BF16)
            Btok = two("Btok", [128, 512], BF16)
            BT = two("BT", [128, 4, 128], BF16)
            CT = two("CT", [128, 4, 128], BF16)
            sm = two("sm", [128, 8, NH])
            scm = two("scm", [128, 4, 128], BF16)
            ysum = two("ysum", [128, DSSD])
            yn = [sa("yn", [128, DSSD], BF16)] * 2
            ycat = [sa("ycat", [128, 12, 128], BF16)] * 2
            Rg2 = sa("Rg2", [128, 12, 128])
            decT2 = sa("decT2", [128, 12, 128], BF16)
            ytmp = two("ytmp", [128, 384])
            prev = sa("prev", [128, DSSD])
            prev_bf = sa("prev_bf", [128, DSSD], BF16)
            ssq = sa("ssq", [128, 8])
            xch = sa("xch", [128, D])
            stats = sa("stats", [128, 2, 6]); mv = sa("mv", [128, 2]); rstd = sa("rstd", [128, 1])
            DMA(xs[0][0:1, 0:1280], convb_d[0:1, 0:1280], [], ["xs_0"])
            cp("dve", convb_bf[0:1, 0:1280], xs[0][0:1, 0:1280], ["xs_0"], ["convb_bf"])
            DMA(xs[0][0:1, 0:1280], convb_d[0:1, 1280:2560], ["xs_0"], ["xs_0"])
            cp("dve", convb_bf[0:1, 1280:2560], xs[0][0:1, 0:1280], ["xs_0"], ["convb_bf"])

            SB_PER_SEQ = SEQ // SBT
            NSBG = 2 * SB_PER_SEQ
            NCHG = NSBG * NCH

            def nm(base, i):
                return "%s_%d" % (base, i)

            fillq = []
            fstate = {"old": 0}

            def fill(n):
                for _ in range(n):
                    if fillq:
                        fillq.pop(0)()
                        fstate["old"] = max(0, fstate["old"] - 1)

            def S0_pieces(sbg):
                b, sbi = sbg // SB_PER_SEQ, sbg % SB_PER_SEQ
                q = sbg % 2
                t0 = sbi * SBT
                FB = 3

                def pre():
                    if sbi == 0:
                        memset("pool", xbcraw[q][:, :, 0:3], 0.0, [nm("xbcraw", q)])
                    else:
                        cp("act", xbcraw[q][:, :, 0:3], xbcraw[1 - q][:, :, SBT:SBT + 3], [nm("xbcraw", 1 - q)], [nm("xbcraw", q)])
                    xTin4 = xch[:].rearrange("p (k t) -> p k t", k=4)
                    for kh in range(2):
                        xn = "xch"
                        DMA(xTin4, xT_d[b, kh * 512:(kh + 1) * 512, t0:t0 + SBT].rearrange("(k p) t -> p k t", p=128), [], [xn])
                        for k4 in range(4):
                            k = kh * 4 + k4
                            ts("dve", u1T[q][:, k, :], xTin4[:, k4, :], mod1T[:, 8 + k, b:b + 1], mod1T[:, k, b:b + 1],
                               ALU.mult, ALU.add, [xn, "mod1T"], [nm("u1T", q)])

                def blk(bi_):
                    slot = bi_ % 2
                    wn = "wbuf%d" % slot
                    col0 = bi_ * WB
                    DMA(wbuf[slot][:], winbf_d[:, :, col0:col0 + WB].rearrange("k p n -> p k n"), ["winbf_scr"], [wn])
                    if col0 < 1536:
                        for c in range(NCH):
                            for k in range(8):
                                mm(PS[FB][:, 0:WB], u1T[q][:, k, c * 128:(c + 1) * 128], wbuf[slot][:, k, :], k == 0, k == 7,
                                   [nm("u1T", q), wn], [pk(FB)])
                            cp("act", zs[q][:, c, col0:col0 + WB], PS[FB][:, 0:WB], [pk(FB)], [nm("zs", q)])
                    else:
                        for ct in range(WB // 128):
                            tile_ = (col0 - 1536) // 128 + ct
                            for k in range(8):
                                mm(PS[FB][:, 0:SBT], wbuf[slot][:, k, ct * 128:(ct + 1) * 128], u1T[q][:, k, :], k == 0, k == 7,
                                   [nm("u1T", q), wn], [pk(FB)])
                            cp("act", xbcraw[q][:, tile_, 3:SBT + 3], PS[FB][:, 0:SBT], [pk(FB)], [nm("xbcraw", q)])

                def dtp():
                    for c in range(NCH):
                        for k in range(8):
                            mm(PS[FB][:, 0:NH], u1T[q][:, k, c * 128:(c + 1) * 128], wdt[:, k, :], k == 0, k == 7, [nm("u1T", q), "wdt"], [pk(FB)])
                        tt("dve", dts[q][:, c, :], PS[FB][:, 0:NH], dtb_b[:], ALU.add, [pk(FB), "dtb_b"], [nm("dts", q)])
                    act(dts[q][:], dts[q][:], AF.Exp, [nm("dts", q)], [nm("dts", q)])
                    act(dts[q][:], dts[q][:], AF.Ln, [nm("dts", q)], [nm("dts", q)], bias=1.0)
                    tt("dve", adts[q][:], dts[q][:], a_b[:].unsqueeze(1).to_broadcast([128, NCH, NH]), ALU.mult, [nm("dts", q), "a_b"], [nm("adts", q)])

                return [pre] + [(lambda i=i: blk(i)) for i in range(6, 16)] + [dtp] + [(lambda i=i: blk(i)) for i in range(0, 6)]

            def S1a(ch):
                sbg, c = ch // NCH, ch % NCH
                q, p = sbg % 2, ch % 2
                w0 = c * 128
                xr, xrn = xbcraw[q], nm("xbcraw", q)
                for grp in range(4):
                    bank = 4 + (grp % 2)
                    mm(PS[bank][:, :], ones_bf[0:1, :], convb_bf[0:1, grp * 512:(grp + 1) * 512], True, False, ["ones_bf", "convb_bf"], [pk(bank)])
                    for t4 in range(4):
                        tile_ = grp * 4 + t4
                        o = PS[bank][:, t4 * 128:(t4 + 1) * 128]
                        for k in range(4):
                            mm(o, xr[:, tile_, w0 + k:w0 + k + 128], dg[:, tile_, k, :], False, (t4 == 3 and k == 3), [xrn, "dg"], [pk(bank)])
                    if grp < 3:
                        act(xs[p][:, grp * 512:(grp + 1) * 512], PS[bank][:, :], AF.Silu, [pk(bank)], [nm("xs", p)])
                    else:
                        act(Btok[p][:], PS[bank][:, :], AF.Silu, [pk(bank)], [nm("Btok", p)])
                for (dst, dname, tb0, bank) in ((BT[p], nm("BT", p), 12, 4), (CT[p], nm("CT", p), 16, 5)):
                    for gq in range(4):
                        tile_ = tb0 + gq
                        o = PS[bank][:, gq * 128:(gq + 1) * 128]
                        for k in range(4):
                            mm(o, dg[:, tile_, k, :], xr[:, tile_, w0 + k:w0 + k + 128], k == 0, k == 3, [xrn, "dg"], [pk(bank)])
                        act(dst[:, gq, :], o, AF.Silu, [pk(bank), "convbT"], [dname], bias=convbT[:, tile_:tile_ + 1])

            def S1b(ch):
                sbg, c = ch // NCH, ch % NCH
                q, p = sbg % 2, ch % 2
                smn = nm("sm", p)
                sm_ = sm[p]
                mm(PS[5][:, 0:NH], triU[:], adts[q][:, c, :], True, True, ["triU", nm("adts", q)], [pk(5)])
                mm(PS[5][:, 32:32 + NH], ones_f[:], adts[q][:, c, :], True, True, ["ones_f", nm("adts", q)], [pk(5)])
                cp("dve", sm_[:, 0, :], PS[5][:, 0:NH], [pk(5)], [smn])
                cp("dve", sm_[:, 1, :], PS[5][:, 32:32 + NH], [pk(5)], [smn])
                tt("dve", sm_[:, 6, :], sm_[:, 1, :], sm_[:, 0, :], ALU.subtract, [smn], [smn])
                act(sm_[:, 2, :], sm_[:, 0, :], AF.Exp, [smn], [smn])
                act(sm_[:, 3, :], sm_[:, 6, :], AF.Exp, [smn], [smn])
                act(sm_[:, 4, :], sm_[:, 1, :], AF.Exp, [smn], [smn])
                tt("dve", sm_[:, 5, :], dts[q][:, c, :], sm_[:, 3, :], ALU.mult, [nm("dts", q), smn], [smn])
                xs3 = xs[p][:].rearrange("p (h d) -> p h d", d=64)
                tt("dve", xdt[p][:].rearrange("p (h d) -> p h d", d=64), xs3, dts[q][:, c, :].unsqueeze(2).to_broadcast([128, NH, 64]),
                   ALU.mult, [nm("xs", p), nm("dts", q)], [nm("xdt", p)])
                tt("dve", xdte[p][:].rearrange("p (h d) -> p h d", d=64), xs3, sm_[:, 5, :].unsqueeze(2).to_broadcast([128, NH, 64]),
                   ALU.mult, [nm("xs", p), smn], [nm("xdte", p)])
                for g in range(4):
                    mm(PS[5][:, g * 128:(g + 1) * 128], BT[p][:, g, :], CT[p][:, g, :], True, True, [nm("BT", p), nm("CT", p)], [pk(5)])
                tt("dve", scm[p][:], PS[5][:, :].rearrange("p (g l) -> p g l", l=128), triU[:].unsqueeze(1).to_broadcast([128, 4, 128]),
                   ALU.mult, [pk(5), "triU"], [nm("scm", p)])

            def S2(ch):
                sbg, c = ch // NCH, ch % NCH
                q, p = sbg % 2, ch % 2
                smn = nm("sm", p)
                sm_ = sm[p]
                if ch % (NCH * SB_PER_SEQ) == 0:
                    memset("pool", prev[:], 0.0, ["prev%d" % g_ for g_ in range(4)]); memset("pool", prev_bf[:], 0.0, ["prev_bf%d" % g_ for g_ in range(4)])
                for pr2 in range(2):
                    h12 = slice(pr2 * 12, pr2 * 12 + 12)
                    tt("dve", Rg2[:], adts[q][:, c, h12].unsqueeze(2).to_broadcast([128, 12, 128]), triU[:].unsqueeze(1).to_broadcast([128, 12, 128]),
                       ALU.mult, [nm("adts", q), "triU"], ["Rg2"])
                    Rf = Rg2[:].rearrange("p j l -> p (j l)")
                    dT = decT2[:].rearrange("p j l -> p (j l)")
                    for i3 in range(3):
                        mm(PS[i3][:, :], strictL[:], Rf[:, i3 * 512:(i3 + 1) * 512], True, True, ["strictL", "Rg2"], [pk(i3)])
                    for i3 in range(3):
                        act(dT[:, i3 * 512:(i3 + 1) * 512], PS[i3][:, :], AF.Exp, [pk(i3)], ["decT2"])
                    tt("dve", decT2[:].rearrange("p (g j) l -> p g j l", g=2), decT2[:].rearrange("p (g j) l -> p g j l", g=2),
                       scm[p][:, 2 * pr2:2 * pr2 + 2, :].unsqueeze(2).to_broadcast([128, 2, 6, 128]), ALU.mult,
                       ["decT2", nm("scm", p)], ["decT2"])
                    fill(1)
                    for gi in range(2):
                        g = 2 * pr2 + gi
                        cs_ = slice(g * 384, g * 384 + 384)
                        yb, ob = 4 + gi, 6 + gi
                        for j in range(6):
                            h = g * 6 + j
                            mm(PS[yb][:, j * 64:(j + 1) * 64], decT2[:, gi * 6 + j, :], xdt[p][:, h * 64:(h + 1) * 64], True, True,
                               ["decT2", nm("xdt", p)], [pk(yb)])
                        mm(PS[ob][:, 0:384], CT[p][:, g, :], prev_bf[:, cs_], True, True, [nm("CT", p), "prev_bf%d" % g], [pk(ob)])
                    for gi in range(2):
                        g = 2 * pr2 + gi
                        hs = slice(g * 6, g * 6 + 6)
                        cs_ = slice(g * 384, g * 384 + 384)
                        yb, ob = 4 + gi, 6 + gi
                        tt("dve", ytmp[gi][:].rearrange("p (j d) -> p j d", d=64), PS[ob][:, 0:384].rearrange("p (j d) -> p j d", d=64),
                           sm_[:, 2, hs].unsqueeze(2).to_broadcast([128, 6, 64]), ALU.mult, [pk(ob), smn], [nm("ytmp", gi)])
                        tt("dve", ysum[p][:, cs_], PS[yb][:, 0:384], ytmp[gi][:], ALU.add, [pk(yb), nm("ytmp", gi)], [nm("ysum", p) + "g%d" % g])
                    fill(1)
                    for gi in range(2):
                        g = 2 * pr2 + gi
                        cs_ = slice(g * 384, g * 384 + 384)
                        ob = 6 + gi
                        mm(PS[ob][:, 0:384], Btok[p][:, g * 128:(g + 1) * 128], xdte[p][:, cs_], True, True, [nm("Btok", p), nm("xdte", p)], [pk(ob)])
                    for gi in range(2):
                        g = 2 * pr2 + gi
                        hs = slice(g * 6, g * 6 + 6)
                        cs_ = slice(g * 384, g * 384 + 384)
                        ob = 6 + gi
                        tt("dve", prev[:, cs_].rearrange("p (j d) -> p j d", d=64), prev[:, cs_].rearrange("p (j d) -> p j d", d=64),
                           sm_[:, 4, hs].unsqueeze(2).to_broadcast([128, 6, 64]), ALU.mult, ["prev%d" % g, smn], ["prev%d" % g])
                        tt("dve", prev[:, cs_], prev[:, cs_], PS[ob][:, 0:384], ALU.add, ["prev%d" % g, pk(ob)], ["prev%d" % g])
                        cp("act", prev_bf[:, cs_], prev[:, cs_], ["prev%d" % g], ["prev_bf%d" % g])

            def S3a(ch):
                sbg, c = ch // NCH, ch % NCH
                q, p = sbg % 2, ch % 2
                xsn, ysn = nm("xs", p), nm("ysum", p)
                act(zs[q][:, c, :], zs[q][:, c, :], AF.Silu, [nm("zs", q)], [nm("zs", q)])
                tt("dve", xs[p][:].rearrange("p (h d) -> p h d", d=64), xs[p][:].rearrange("p (h d) -> p h d", d=64), dfull[:].unsqueeze(2).to_broadcast([128, NH, 64]), ALU.mult, [xsn, "dfull"], [xsn])
                tt("dve", ysum[p][:], ysum[p][:], xs[p][:], ALU.add, [ysn, xsn] + [ysn + "g%d" % g_ for g_ in range(4)], [ysn] + [ysn + "g%d" % g_ for g_ in range(4)])
                tt("dve", ysum[p][:], ysum[p][:], zs[q][:, c, :], ALU.mult, [ysn, nm("zs", q)], [ysn])
                memset("dve", ssq[:, 0:4], 0.0, ["ssq"])
                for g in range(4):
                    cs_ = slice(g * 384, g * 384 + 384)
                    r_ = g % 2
                    A(lambda e, o=ytmp[r_][:], i_=ysum[p][:, cs_], acc=ssq[:, g:g + 1]: e.activation(out=o, in_=i_, func=AF.Square, accum_out=acc),
                      [ysn, "ssq"], [nm("ytmp", r_), "ssq"])
                ts("dve", ssq[:, 4:8], ssq[:, 0:4], 1.0 / 384.0, EPS, ALU.mult, ALU.add, ["ssq"], ["ssq"])
                act(ssq[:, 4:8], ssq[:, 4:8], AF.Ln, ["ssq"], ["ssq"])
                act(ssq[:, 4:8], ssq[:, 4:8], AF.Exp, ["ssq"], ["ssq"], scale=-0.5)
                tt("dve", yn[p][:].rearrange("p (g d) -> p g d", d=384), ysum[p][:].rearrange("p (g d) -> p g d", d=384),
                   ssq[:, 4:8].unsqueeze(2).to_broadcast([128, 4, 384]), ALU.mult, [ysn, "ssq"], ["yn"])

            def S3b(ch):
                sbg, c = ch // NCH, ch % NCH
                q, p = sbg % 2, ch % 2
                b, sbi = sbg // SB_PER_SEQ, sbg % SB_PER_SEQ
                t0 = sbi * SBT
                w0 = c * 128
                xsn, ysn = nm("xs", p), nm("ysum", p)
                fill(2)
                for t3 in range(3):
                    bank = 4 + (t3 % 2)
                    for t4 in range(4):
                        tile_ = t3 * 4 + t4
                        T(lambda e, o=PSB[bank][:, t4 * 128:(t4 + 1) * 128], i_=yn[p][:, tile_ * 128:(tile_ + 1) * 128]:
                          e.transpose(out=o, in_=i_, identity=ident_bf[:]), ["yn", "ident_bf"], [pk(bank)])
                    cp("act" if t3 % 2 else "dve", ycat[p][:, t3 * 4:t3 * 4 + 4, :],
                       PSB[bank][:, 0:512].rearrange("p (t l) -> p t l", l=128), [pk(bank)], ["ycat"])
                rr_ = xs[p]
                x1t = ysum[p]
                DMA(xch[:], x_d[b, t0 + w0:t0 + w0 + 128, :], [], ["xch"])
                DMA(ys5T[p][:], ys5_d[b, :, :, t0 + w0:t0 + w0 + 128].rearrange("q p t -> p q t"), ["ys5_scr"], [nm("ys5T", p)])
                for half in range(2):
                    bank = 6 + half
                    for kt in range(16):
                        if kt < 12:
                            lhs, ln_ = ycat[p][:, kt, :], "ycat"
                        else:
                            lhs, ln_ = ys5T[p][:, kt - 12, :], nm("ys5T", p)
                        mm(PS[bank][:, :], lhs, wout_bf[:, kt, half * 512:(half + 1) * 512], kt == 0, kt == 15,
                           [ln_, "wout_bf"], [pk(bank)])
                    hsl = slice(half * 512, (half + 1) * 512)
                    tt("dve", rr_[:, hsl], PS[bank][:, :], g1p1[:, b, hsl], ALU.mult, [pk(bank), "g1p1"], [xsn])
                    stt(rr_[:, hsl], xch[:, hsl], ALPHA, rr_[:, hsl], ALU.mult, ALU.add, ["xch", xsn], [xsn])
                    V(lambda e, o=stats[:, half, :], i_=rr_[:, hsl]: e.bn_stats(out=o, in_=i_), [xsn], ["stats"])
                V(lambda e: e.bn_aggr(out=mv[:], in_=stats[:].rearrange("p a s -> p (a s)")), ["stats"], ["mv"])
                ts("dve", rstd[:], mv[:, 1:2], EPS, None, ALU.add, None, ["mv"], ["rstd"])
                act(rstd[:], rstd[:], AF.Ln, ["rstd"], ["rstd"])
                act(rstd[:], rstd[:], AF.Exp, ["rstd"], ["rstd"], scale=-0.5)
                fill(2)
                ts("dve", rr_[:, 0:D], rr_[:, 0:D], mv[:, 0:1], rstd[:, 0:1], ALU.subtract, ALU.mult, [xsn, "mv", "rstd"], [xsn])
                tt("dve", rr_[:, 0:D], rr_[:, 0:D], ln1g_b[:], ALU.mult, [xsn, "ln1g_b"], [xsn])
                tt("dve", x1t[:, 0:D], rr_[:, 0:D], ln1b_b[:], ALU.add, [xsn, "ln1b_b"], [ysn])
                DMA(x1_d[b, t0 + w0:t0 + w0 + 128, :], x1t[:, 0:D], [ysn], ["x1_scr"], chan=("x1w", p), q="pool")

            for pc_ in S0_pieces(0):
                pc_()
            S1a(0)
            S1b(0)
            for ch in range(NCHG):
                nxt = ch + 1
                if ch % NCH == 0 and (ch // NCH + 1) < NSBG:
                    fstate["old"] = len(fillq)
                    fillq.extend(S0_pieces(ch // NCH + 1))
                    fill(fstate["old"] + 1)
                S2(ch)
                if nxt < NCHG:
                    if nxt % NCH == 0:
                        fill(max(0, len(fillq) - 6))
                    S1a(nxt)
                S3a(ch)
                if nxt < NCHG:
                    S1b(nxt)
                S3b(ch)
            fill(len(fillq))
            S.barrier(mk_bar, ["x1_scr"])

        if "B" in phases:
          with contextlib.ExitStack() as stB:
            def sbb(name, shape, dt=F32):
                return stB.enter_context(nc.sbuf_tensor("t%d_%s" % (next(_uid), name), list(shape), dt))

            w1_bf = sbb("w1_bf", [128, 8, DFF], BF16)
            w2_bf = sbb("w2_bf", [128, 32, D], BF16)
            wld = [sbb("wld%d" % i, [128, 512]) for i in range(3)]
            b1T = sbb("b1T", [128, 32])
            DMA(b1T[:], b1T_d[:, :], [], ["b1T"])
            b2bf = sbb("b2bf", [1, D], BF16)
            ln2g_b = sbb("ln2g_b", [128, D]); ln2b_b = sbb("ln2b_b", [128, D])
            DMA(ln2g_b[:], ln2g_d[0:1, :].partition_broadcast(128), [], ["ln2g_b"])
            DMA(ln2b_b[:], ln2b_d[0:1, :].partition_broadcast(128), [], ["ln2b_b"])
            modU = sbb("modU", [128, 2, D])
            modG = sbb("modG", [128, D])
            n_ = 0
            for k in range(8):
                for hf in range(8):
                    wn = "wld%d" % (n_ % 3)
                    DMA(wld[n_ % 3][:], w1_d[k * 128:(k + 1) * 128, hf * 512:(hf + 1) * 512], [], [wn])
                    cp("dve" if (n_ % 2) else "act", w1_bf[:, k, hf * 512:(hf + 1) * 512], wld[n_ % 3][:], [wn], ["w1_bf"])
                    n_ += 1
            for f2 in range(64):
                wn = "wld%d" % (n_ % 3)
                fq, fh = f2 // 2, f2 % 2
                DMA(wld[n_ % 3][:], w2_d[fq * 128:(fq + 1) * 128, fh * 512:(fh + 1) * 512], [], [wn])
                cp("dve" if (n_ % 2) else "act", w2_bf[:, fq, fh * 512:(fh + 1) * 512], wld[n_ % 3][:], [wn], ["w2_bf"])
                n_ += 1
            x1b_ = [sbb("x1b%d" % i, [128, 2, D]) for i in range(2)]
            u2 = sbb("u2", [128, D], BF16)
            u2T_ = [sbb("u2T%d" % i, [128, 8, 256], BF16) for i in range(2)]
            hT = sbb("hT", [128, 32, 256], BF16)
            htmp = [sbb("htmp%d" % i, [128, 256]) for i in range(2)]
            r2 = sbb("r2", [128, D]); ot = r2
            stats2 = sbb("stats2", [128, 2, 6]); mv2 = sbb("mv2", [128, 2]); rstd2 = sbb("rstd2", [128, 1])
            NBLK = 2 * NSB * 2
            DMA(r2[0:1, :], b2_d[0:1, :], [], ["r2"])
            cp("dve", b2bf[:], r2[0:1, :], ["r2"], ["b2bf"])

            def blk_info(i):
                b = i // (NSB * 2)
                t0 = (i % (NSB * 2)) * 256
                return b, t0, i % 2

            def B0_load(i):
                b, t0, pa = blk_info(i)
                DMA(x1b_[pa][:], x1_d[b, t0:t0 + 256, :].rearrange("(c p) d -> p c d", p=128), ["x1_scr"], ["x1b%d" % pa])

            def B0_u2(i, c):
                b, t0, pa = blk_info(i)
                if t0 == 0 and c == 0:
                    DMA(modU[:], modb_d[:, (b * 4 + 1) * D:(b * 4 + 3) * D].rearrange("p (q d) -> p q d", d=D), ["modb_scr"], ["modU"])
                tt("dve", u2[:], x1b_[pa][:, c, :], modU[:, 1, :], ALU.mult, ["x1b%d" % pa, "modU"], ["u2"])
                tt("dve", u2[:], u2[:], modU[:, 0, :], ALU.add, ["u2", "modU"], ["u2"])
                for kq in range(2):
                    bank = kq
                    for k4 in range(4):
                        k = kq * 4 + k4
                        T(lambda e, o=PSB[bank][:, k4 * 128:(k4 + 1) * 128], i_=u2[:, k * 128:(k + 1) * 128]:
                          e.transpose(out=o, in_=i_, identity=ident_bf[:]), ["u2", "ident_bf"], [pk(bank)])
                    cp("act", u2T_[pa][:, kq * 4:kq * 4 + 4, c * 128:(c + 1) * 128],
                       PSB[bank][:, 0:512].rearrange("p (t l) -> p t l", l=128), [pk(bank)], ["u2T%d" % pa])

            def B1(i):
                b, t0, pa = blk_info(i)
                for f in range(32):
                    bank = 2 + (f % 4)
                    for k in range(8):
                        mm(PS[bank][:, 0:256], w1_bf[:, k, f * 128:(f + 1) * 128], u2T_[pa][:, k, :], k == 0, k == 7, ["w1_bf", "u2T%d" % pa], [pk(bank)])
                    hn = "htmp%d" % (f % 2)
                    act(htmp[f % 2][:], PS[bank][:, 0:256], AF.Identity, [pk(bank), "b1T"], [hn], bias=b1T[:, f:f + 1])
                    stt(hT[:, f, :], htmp[f % 2][:], 0.0, htmp[f % 2][:], ALU.max, ALU.mult, [hn], ["hT"])
                    if i + 1 < NBLK and f in (6, 20):
                        B0_u2(i + 1, 0 if f == 6 else 1)

            def B2(i):
                b, t0, pa = blk_info(i)
                x1b = x1b_[pa]
                xn = "x1b%d" % pa
                if t0 == 0:
                    DMA(modG[:], modb_d[:, (b * 4 + 3) * D:(b * 4 + 4) * D], ["modb_scr"], ["modG"])
                for c in range(2):
                    for half in range(2):
                        bank = 6 + half
                        hsl = slice(half * 512, (half + 1) * 512)
                        for f in range(32):
                            mm(PS[bank][:, :], hT[:, f, c * 128:(c + 1) * 128], w2_bf[:, f, hsl], f == 0, False, ["hT", "w2_bf"], [pk(bank)])
                        mm(PS[bank][:, :], ones_bf[0:1, :], b2bf[0:1, hsl], False, True, ["ones_bf", "b2bf"], [pk(bank)])
                        tt("dve", r2[:, hsl], PS[bank][:, :], modG[:, hsl], ALU.mult, [pk(bank), "modG"], ["r2"])
                        stt(r2[:, hsl], x1b[:, c, hsl], ALPHA, r2[:, hsl], ALU.mult, ALU.add, [xn, "r2"], ["r2"])
                        V(lambda e, o=stats2[:, half, :], i_=r2[:, hsl]: e.bn_stats(out=o, in_=i_), ["r2"], ["stats2"])
                    V(lambda e: e.bn_aggr(out=mv2[:], in_=stats2[:].rearrange("p a s -> p (a s)")), ["stats2"], ["mv2"])
                    ts("dve", rstd2[:], mv2[:, 1:2], EPS, None, ALU.add, None, ["mv2"], ["rstd2"])
                    act(rstd2[:], rstd2[:], AF.Ln, ["rstd2"], ["rstd2"])
                    act(rstd2[:], rstd2[:], AF.Exp, ["rstd2"], ["rstd2"], scale=-0.5)
                    ts("dve", r2[:], r2[:], mv2[:, 0:1], rstd2[:, 0:1], ALU.subtract, ALU.mult, ["r2", "mv2", "rstd2"], ["r2"])
                    tt("dve", r2[:], r2[:], ln2g_b[:], ALU.mult, ["r2", "ln2g_b"], ["r2"])
                    tt("dve", ot[:], r2[:], ln2b_b[:], ALU.add, ["r2", "ln2b_b"], ["r2"])
                    DMA(out_d[b, t0 + c * 128:t0 + (c + 1) * 128, :], ot[:], ["r2"], [], chan=("outw",), q="pool")

            B0_load(0)
            B0_u2(0, 0)
            B0_u2(0, 1)
            for i in range(NBLK):
                if i + 1 < NBLK:
                    B0_load(i + 1)
                B1(i)
                B2(i)
            out_chans.append(("outw",))

        if "x1" in dbg_d:
            pass
        S.finalize()
        S.emit(final_wait_chans=out_chans)
    return nc


def _l1(a):
    a = np.asarray(a)
    rest = a.shape[2:]
    a = a.reshape((16, 2, 64) + rest)
    perm = (1, 2, 0) + tuple(range(3, 3 + len(rest)))
    return np.ascontiguousarray(a.transpose(perm).reshape((128, 16) + rest))


def make_in_maps(inputs, NSB=8, n_cores=8):
    f = lambda a: np.ascontiguousarray(np.asarray(a, dtype=np.float32))
    SEQ = NSB * 512
    x = f(inputs["x"])[:, :SEQ]
    c = f(inputs["c"])
    shared = {
        "w_ada": f(inputs["w_ada"][0]),
        "b_ada": f(inputs["b_ada"][0]).reshape(1, -1),
        "b_adaT": f(f(inputs["b_ada"][0])[:2048].reshape(16, 128).T),
        "w_in": f(inputs["w_in"][0]),
        "convwT": f(f(inputs["conv_w"][0]).reshape(4, 20, 128).transpose(2, 1, 0)),
        "conv_b": f(inputs["conv_b"][0]).reshape(1, -1),
        "convbT": f(f(inputs["conv_b"][0]).reshape(20, 128).T),
        "dt_bias": f(inputs["dt_bias"][0]).reshape(1, -1),
        "a_log": f(inputs["a_log"][0]).reshape(1, -1),
        "dfull": f(np.repeat(f(inputs["d_ssd"][0]), 64)).reshape(1, -1),
        "d_ssd": f(inputs["d_ssd"][0]).reshape(1, -1),
        "normwT": f(f(inputs["norm_w"][0]).reshape(12, 128).T),
        "s5ar": _l1(f(inputs["s5_a_re"][0])),
        "s5ai": _l1(f(inputs["s5_a_im"][0])),
        "s5ldt": _l1(np.repeat(f(inputs["s5_log_dt"][0])[:, None], 64, axis=1)),
        "s5br": _l1(f(inputs["s5_b_re"][0])),
        "s5bi": _l1(f(inputs["s5_b_im"][0])),
        "s5cr": _l1(f(inputs["s5_c_re"][0]).transpose(0, 2, 1)),
        "s5ci": _l1(f(inputs["s5_c_im"][0]).transpose(0, 2, 1)),
        "s5dv": f(np.tile(f(inputs["s5_d"][0]).T, (8, 1))),
        "w_glu": f(inputs["w_glu"][0]),
        "b_gluT": f(f(inputs["b_glu"][0]).reshape(4, 128).T),
        "w_out": f(inputs["w_out"][0]),
        "ln1_g": f(inputs["ln1_g"][0]).reshape(1, -1),
        "ln1_b": f(inputs["ln1_b"][0]).reshape(1, -1),
        "w1": f(inputs["w1"][0]),
        "b1T": f(f(inputs["b1"][0]).reshape(32, 128).T),
        "w2": f(inputs["w2"][0]),
        "b2": f(inputs["b2"][0]).reshape(1, -1),
        "ln2_g": f(inputs["ln2_g"][0]).reshape(1, -1),
        "ln2_b": f(inputs["ln2_b"][0]).reshape(1, -1),
    }
    maps = []
    for i in range(n_cores):
        xb = x[2 * i:2 * i + 2]
        m = dict(shared)
        m["x"] = np.ascontiguousarray(xb)
        m["xT"] = np.ascontiguousarray(xb.transpose(0, 2, 1))
        m["cT"] = np.ascontiguousarray(c[2 * i:2 * i + 2].reshape(2, 8, 128).transpose(2, 1, 0))
        maps.append(m)
    return maps


_NC_CACHE = {}


def kernel(**inputs):
    NSB = 8
    if NSB not in _NC_CACHE:
        _NC_CACHE[NSB] = build_program(NSB)
    nc = _NC_CACHE[NSB]
    maps = make_in_maps(inputs, NSB)
    res = run_bass_kernel_spmd(nc, maps, core_ids=list(range(8)))
    out = np.concatenate([np.asarray(r["out"]) for r in res.results], axis=0)
    return out.astype(np.float32)
```
